# Optimizing a Trainium2 kernel written in Bass

```python
import math
import jax, jax.numpy as jnp
from jax import lax
import numpy as np

D_MODEL = 2048
BATCH = 16
SEQ = 256
DEPTH = 4
DEC_BATCH = 8
DEC_SEQ = 4096
PAST_LEN = 512

GRID_W = 64
N_A_LAYERS = (DEPTH + 1) // 2
N_D_LAYERS = DEPTH // 2
MLA_HEADS = 12
MLA_NOPE = 128
MLA_ROPE = 64
MLA_V = 128
Q_LORA = 512
KV_LORA = 512
MLA_SCALE = 1.0 / math.sqrt(MLA_NOPE + MLA_ROPE)
FNET_GROUPS = 4
FNET_GC = 128
FNET_WIDTH = FNET_GROUPS * FNET_GC
IN_A = Q_LORA + KV_LORA + MLA_ROPE + FNET_WIDTH
POOL_WINDOWS = (2, 4, 8, 16)
POOL_GC = 128
POOL_WIDTH = len(POOL_WINDOWS) * POOL_GC
DIFF_HEADS = 12
DIFF_DH = 64
DIFF_QK = DIFF_HEADS * 2 * DIFF_DH
DIFF_VW = DIFF_HEADS * 2 * DIFF_DH
DIFF_SCALE = 1.0 / math.sqrt(DIFF_DH)
IN_D = POOL_WIDTH + 2 * DIFF_QK + DIFF_VW
MIX_WIDTH = 2048
D_FF = 5632
N_MOD = 9
ROPE_BASE = 10000.0
EPS = 1e-6
Q_BLOCK = 128

kernel_name = "hybrid_mla_fnet_pool_diffattn_dit_step"


def rmsnorm(x, g):
    xf = x.astype(jnp.float32)
    y = xf * lax.rsqrt(jnp.mean(xf * xf, axis=-1, keepdims=True) + EPS)
    return (y * g.astype(jnp.float32)).astype(x.dtype)


def grid_positions(rows):
    row = jnp.repeat(jnp.arange(rows), GRID_W)
    col = jnp.tile(jnp.arange(GRID_W), rows)
    return row, col


def rope_axis(x, pos):
    half = x.shape[-1] // 2
    inv = ROPE_BASE ** (-jnp.arange(half, dtype=jnp.float32) / half)
    ang = pos.astype(jnp.float32)[:, None] * inv
    ang = ang.reshape((ang.shape[0],) + (1,) * (x.ndim - 3) + (half,))
    cos, sin = jnp.cos(ang), jnp.sin(ang)
    x1, x2 = x[..., :half], x[..., half:]
    return jnp.concatenate([x1 * cos - x2 * sin, x2 * cos + x1 * sin], axis=-1).astype(x.dtype)


def rope_2d(x, row, col):
    h = x.shape[-1] // 2
    return jnp.concatenate([rope_axis(x[..., :h], row), rope_axis(x[..., h:], col)], axis=-1)


def modulate(h, shift, scale):
    return h * (1 + scale) + shift


def swiglu(h, wg, wu, wd):
    return (jax.nn.silu(h @ wg) * (h @ wu)) @ wd


def over_query_blocks(fn, qs):
    B, S = qs[0].shape[:2]
    qb = Q_BLOCK if S % Q_BLOCK == 0 else S
    nb = S // qb
    blocks = tuple(jnp.swapaxes(q.reshape((B, nb, qb) + q.shape[2:]), 0, 1) for q in qs)
    out = lax.map(lambda b: fn(*b), blocks)
    out = jnp.swapaxes(out, 0, 1)
    return out.reshape((B, S) + out.shape[3:])


def mla_block(qn, qr, kn, kr, v):
    s = (jnp.einsum('bqhd,bkhd->bhqk', qn, kn) + jnp.einsum('bqhr,bkr->bhqk', qr, kr)) * MLA_SCALE
    p = jax.nn.softmax(s.astype(jnp.float32), axis=-1).astype(v.dtype)
    return jnp.einsum('bhqk,bkhd->bqhd', p, v)


def diff_block(q, k, v, lam):
    s = jnp.einsum('bqhcd,bkhcd->bhcqk', q, k) * DIFF_SCALE
    p = jax.nn.softmax(s.astype(jnp.float32), axis=-1)
    a = (p[:, :, 0] - lam * p[:, :, 1]).astype(v.dtype)
    return jnp.einsum('bhqk,bkhe->bqhe', a, v)


def fourier_mix(fz, w_fnet):
    B, S, _ = fz.shape
    z = fz.reshape(B, S, FNET_GROUPS, FNET_GC).astype(jnp.float32)
    f = jnp.fft.fft2(z, axes=(1, 3), norm='ortho').real.astype(fz.dtype)
    return jnp.einsum('bsgc,gcd->bsgd', f, w_fnet).reshape(B, S, FNET_WIDTH)


def pool_mix(pz, w_pool, pool_scale):
    B, S, _ = pz.shape
    t = jnp.arange(S)
    outs = []
    for g, w in enumerate(POOL_WINDOWS):
        xg = pz[..., g * POOL_GC:(g + 1) * POOL_GC].astype(jnp.float32)
        c0 = jnp.pad(jnp.cumsum(xg, axis=1), ((0, 0), (1, 0), (0, 0)))
        lo = jnp.clip(t - w // 2, 0, S)
        hi = jnp.clip(t + w - w // 2, 0, S)
        mean = (c0[:, hi] - c0[:, lo]) / (hi - lo).astype(jnp.float32)[:, None]
        outs.append(mean - xg)
    pooled = jnp.stack(outs, axis=2).astype(pz.dtype)
    y = jnp.einsum('bsgc,gcd->bsgd', pooled, w_pool).reshape(B, S, POOL_WIDTH)
    return y * pool_scale


def mixer_mla_fourier(h, w_in, g_q, g_kv, w_q_up, w_kv_up, w_fnet, w_o, pos, ctx):
    B, S, _ = h.shape
    u = h @ w_in
    q_lat, ckv, krope, fz = jnp.split(u, [Q_LORA, Q_LORA + KV_LORA, Q_LORA + KV_LORA + MLA_ROPE], axis=-1)
    q = (rmsnorm(q_lat, g_q) @ w_q_up).reshape(B, S, MLA_HEADS, MLA_NOPE + MLA_ROPE)
    qn, qr = q[..., :MLA_NOPE], q[..., MLA_NOPE:]
    ckv = rmsnorm(ckv, g_kv)
    state = (ckv, krope)
    if pos is not None:
        row, col = pos
        qr = rope_2d(qr, row, col)
        ckv_all = jnp.concatenate([ckv, ctx[0]], axis=1)
        kr_all = jnp.concatenate([rope_2d(krope, row, col), ctx[1]], axis=1)
    else:
        ckv_all, kr_all = ckv, krope
    K = ckv_all.shape[1]
    kv = (ckv_all @ w_kv_up).reshape(B, K, MLA_HEADS, MLA_NOPE + MLA_V)
    kn, v = kv[..., :MLA_NOPE], kv[..., MLA_NOPE:]
    attn = over_query_blocks(lambda a, b: mla_block(a, b, kn, kr_all, v), (qn, qr))
    attn = attn.reshape(B, S, MLA_HEADS * MLA_V)
    four = fourier_mix(fz, w_fnet)
    return jnp.concatenate([attn, four], axis=-1) @ w_o, state


def mixer_pool_diff(h, w_in, lam_qk, g_sub, w_pool, pool_scale, w_o, lam_init, pos, ctx):
    B, S, _ = h.shape
    u = h @ w_in
    pz, q, k, v = jnp.split(u, [POOL_WIDTH, POOL_WIDTH + DIFF_QK, POOL_WIDTH + 2 * DIFF_QK], axis=-1)
    q = q.reshape(B, S, DIFF_HEADS, 2, DIFF_DH)
    k = k.reshape(B, S, DIFF_HEADS, 2, DIFF_DH)
    v = v.reshape(B, S, DIFF_HEADS, 2 * DIFF_DH)
    state = (k, v)
    if pos is not None:
        row, col = pos
        q = rope_2d(q, row, col)
        k_all = jnp.concatenate([rope_2d(k, row, col), ctx[0]], axis=1)
        v_all = jnp.concatenate([v, ctx[1]], axis=1)
    else:
        k_all, v_all = k, v
    lf = lam_qk.astype(jnp.float32)
    lam = jnp.exp(jnp.sum(lf[0] * lf[1])) - jnp.exp(jnp.sum(lf[2] * lf[3])) + lam_init
    o = over_query_blocks(lambda a: diff_block(a, k_all, v_all, lam), (q,))
    o = (rmsnorm(o, g_sub) * (1.0 - lam_init)).reshape(B, S, DIFF_VW)
    pool = pool_mix(pz, w_pool, pool_scale)
    return jnp.concatenate([pool, o], axis=-1) @ w_o, state


def trunk(x, cond, pos, caches, p):
    ckv_l, kr_l, dk_l, dv_l = [], [], [], []
    for l in range(DEPTH):
        mod = (jax.nn.silu(cond) @ p['w_mod'][l] + p['b_mod'][l]).reshape(cond.shape[0], 1, N_MOD, D_MODEL)
        m = [mod[:, :, i] for i in range(N_MOD)]
        h = modulate(rmsnorm(x, p['g_norm'][l, 0]), m[0], m[1])
        x = x + 0.5 * m[2] * swiglu(h, p['w_ffn_gate'][l, 0], p['w_ffn_up'][l, 0], p['w_ffn_down'][l, 0])
        h = modulate(rmsnorm(x, p['g_norm'][l, 1]), m[3], m[4])
        i = l // 2
        if l % 2 == 0:
            ctx = None if caches is None else (caches[0][:, i], caches[1][:, i])
            out, st = mixer_mla_fourier(h, p['w_in_a'][i], p['g_q'][i], p['g_kv'][i], p['w_q_up'][i],
                                        p['w_kv_up'][i], p['w_fnet'][i], p['w_o_a'][i], pos, ctx)
            ckv_l.append(st[0]); kr_l.append(st[1])
        else:
            ctx = None if caches is None else (caches[2][:, i], caches[3][:, i])
            lam_init = 0.8 - 0.6 * math.exp(-0.3 * l)
            out, st = mixer_pool_diff(h, p['w_in_d'][i], p['lam_qk'][i], p['g_sub'][i], p['w_pool'][i],
                                      p['pool_scale'][i], p['w_o_d'][i], lam_init, pos, ctx)
            dk_l.append(st[0]); dv_l.append(st[1])
        x = x + m[5] * out
        h = modulate(rmsnorm(x, p['g_norm'][l, 2]), m[6], m[7])
        x = x + 0.5 * m[8] * swiglu(h, p['w_ffn_gate'][l, 1], p['w_ffn_up'][l, 1], p['w_ffn_down'][l, 1])
    y = rmsnorm(x, p['g_final'])
    states = (jnp.stack(ckv_l, axis=1), jnp.stack(kr_l, axis=1),
              jnp.stack(dk_l, axis=1), jnp.stack(dv_l, axis=1))
    return y, states


def setup_inputs(seed: int = 0) -> dict:
    key = jax.random.key(seed)
    ks = jax.random.split(key, 32)
    f32 = jnp.float32

    def nrm(k, shape, scale=1.0):
        return jax.random.normal(k, shape, f32) * scale

    D, F = D_MODEL, D_FF
    return {
        'x_prompt': nrm(ks[0], (BATCH, SEQ, D)),
        'x_sample': nrm(ks[1], (DEC_BATCH, DEC_SEQ, D)),
        'cache_mla_ckv': nrm(ks[2], (DEC_BATCH, N_A_LAYERS, PAST_LEN, KV_LORA)),
        'cache_mla_krope': nrm(ks[3], (DEC_BATCH, N_A_LAYERS, PAST_LEN, MLA_ROPE)),
        'cache_diff_k': nrm(ks[4], (DEC_BATCH, N_D_LAYERS, PAST_LEN, DIFF_HEADS, 2, DIFF_DH)),
        'cache_diff_v': nrm(ks[5], (DEC_BATCH, N_D_LAYERS, PAST_LEN, DIFF_HEADS, 2 * DIFF_DH)),
        'c': nrm(ks[6], (DEC_BATCH, D)),
        'c_ctx': nrm(ks[7], (D,)),
        'w_mod': nrm(ks[8], (DEPTH, D, N_MOD * D), 0.5 * D ** -0.5),
        'b_mod': nrm(ks[9], (DEPTH, N_MOD * D), 0.02),
        'g_norm': 1.0 + nrm(ks[10], (DEPTH, 3, D), 0.02),
        'w_ffn_gate': nrm(ks[11], (DEPTH, 2, D, F), D ** -0.5),
        'w_ffn_up': nrm(ks[12], (DEPTH, 2, D, F), D ** -0.5),
        'w_ffn_down': nrm(ks[13], (DEPTH, 2, F, D), F ** -0.5),
        'w_in_a': nrm(ks[14], (N_A_LAYERS, D, IN_A), D ** -0.5),
        'g_q': 1.0 + nrm(ks[15], (N_A_LAYERS, Q_LORA), 0.02),
        'g_kv': 1.0 + nrm(ks[16], (N_A_LAYERS, KV_LORA), 0.02),
        'w_q_up': nrm(ks[17], (N_A_LAYERS, Q_LORA, MLA_HEADS * (MLA_NOPE + MLA_ROPE)), Q_LORA ** -0.5),
        'w_kv_up': nrm(ks[18], (N_A_LAYERS, KV_LORA, MLA_HEADS * (MLA_NOPE + MLA_V)), KV_LORA ** -0.5),
        'w_fnet': nrm(ks[19], (N_A_LAYERS, FNET_GROUPS, FNET_GC, FNET_GC), FNET_GC ** -0.5),
        'w_o_a': nrm(ks[20], (N_A_LAYERS, MIX_WIDTH, D), MIX_WIDTH ** -0.5),
        'w_in_d': nrm(ks[21], (N_D_LAYERS, D, IN_D), D ** -0.5),
        'lam_qk': nrm(ks[22], (N_D_LAYERS, 4, DIFF_DH), 0.1),
        'g_sub': 1.0 + nrm(ks[23], (N_D_LAYERS, 2 * DIFF_DH), 0.02),
        'w_pool': nrm(ks[24], (N_D_LAYERS, len(POOL_WINDOWS), POOL_GC, POOL_GC), POOL_GC ** -0.5),
        'pool_scale': 1.0 + nrm(ks[25], (N_D_LAYERS, POOL_WIDTH), 0.02),
        'w_o_d': nrm(ks[26], (N_D_LAYERS, MIX_WIDTH, D), MIX_WIDTH ** -0.5),
        'g_final': 1.0 + nrm(ks[27], (D,), 0.02),
    }


def reference(x_prompt, x_sample, cache_mla_ckv, cache_mla_krope, cache_diff_k, cache_diff_v, c, c_ctx,
              w_mod, b_mod, g_norm, w_ffn_gate, w_ffn_up, w_ffn_down,
              w_in_a, g_q, g_kv, w_q_up, w_kv_up, w_fnet, w_o_a,
              w_in_d, lam_qk, g_sub, w_pool, pool_scale, w_o_d, g_final):
    p = dict(w_mod=w_mod, b_mod=b_mod, g_norm=g_norm, w_ffn_gate=w_ffn_gate, w_ffn_up=w_ffn_up,
             w_ffn_down=w_ffn_down, w_in_a=w_in_a, g_q=g_q, g_kv=g_kv, w_q_up=w_q_up, w_kv_up=w_kv_up,
             w_fnet=w_fnet, w_o_a=w_o_a, w_in_d=w_in_d, lam_qk=lam_qk, g_sub=g_sub, w_pool=w_pool,
             pool_scale=pool_scale, w_o_d=w_o_d, g_final=g_final)
    y_prompt, st = trunk(x_prompt, c_ctx[None, :], None, None, p)
    new_mla_ckv, new_mla_krope, new_diff_k, new_diff_v = st
    rows = x_sample.shape[1] // GRID_W
    pos = grid_positions(rows)
    y_sample, _ = trunk(x_sample, c, pos, (cache_mla_ckv, cache_mla_krope, cache_diff_k, cache_diff_v), p)
    return (y_prompt, y_sample, new_mla_ckv, new_mla_krope, new_diff_k, new_diff_v)
```

```python
import math
import numpy as np
import concourse.bass as bass
import concourse.mybir as mybir
from concourse.bass_utils import run_bass_kernel_spmd

F32, BF16 = mybir.dt.float32, mybir.dt.bfloat16
ALU = mybir.AluOpType
AF = mybir.ActivationFunctionType
AX = mybir.AxisListType

D = 2048; KC = 16; T = 512; DFF = 5632; JC = 44; NMOD = 9
SS = 4096; SP = 256; PAST = 512; NTOK = 4608
EPS = 1e-6
MLA_SCALE = 1.0 / math.sqrt(192.0)
DIFF_SCALE = 1.0 / 8.0
KQ = 12
NWS = 5
WSZ = 4096


class Buf:
    def __init__(self, name, ap):
        self.name, self.ap, self.w, self.r = name, ap, None, []

    def __getitem__(self, idx):
        return V(self, self.ap[idx])


class V:
    def __init__(self, buf, ap):
        self.buf, self.ap = buf, ap

    def __getitem__(self, idx):
        return V(self.buf, self.ap[idx])

    def re(self, pat, **kw):
        return V(self.buf, self.ap.rearrange(pat, **kw))


class Op:
    __slots__ = ("eng", "fn", "deps", "sig", "cnt", "dma_i", "idx")

    def __init__(self, eng, fn):
        self.eng, self.fn, self.deps, self.sig, self.cnt, self.dma_i = eng, fn, [], False, 0, None


ENGS = ("pe", "act", "dve", "pool", "sp")
DMAQ = ("pool", "sp")


class Kern:
    def __init__(self, nc):
        self.nc = nc
        self.ops = {e: [] for e in ENGS}
        self.ndma = {q: 0 for q in DMAQ}
        self.dma_ops = {q: [] for q in DMAQ}
        self.bufs = []

    def sb(self, name, shape, dt):
        t = self.nc.alloc_sbuf_tensor(name, shape, dt)
        b = Buf(name, t.ap() if hasattr(t, "ap") else t[:])
        self.bufs.append(b)
        return b

    def psb(self, name, shape, dt=F32):
        t = self.nc.alloc_psum_tensor(name, shape, dt)
        b = Buf(name, t.ap() if hasattr(t, "ap") else t[:])
        self.bufs.append(b)
        return b

    def add(self, eng, fn, reads=(), writes=()):
        op = Op(eng, fn)
        op.idx = len(self.ops[eng])
        deps = []
        for b in reads:
            if b is not None and b.w is not None:
                deps.append(b.w)
        for b in writes:
            if b is None:
                continue
            if b.w is not None:
                deps.append(b.w)
            deps.extend(b.r)
        if eng in DMAQ:
            i = self.ndma[eng]
            op.dma_i = i
            self.ndma[eng] += 1
            if i >= KQ:
                deps.append(self.dma_ops[eng][i - KQ])
            self.dma_ops[eng].append(op)
        best = {}
        seen = set()
        for d in deps:
            if d is op or id(d) in seen:
                continue
            seen.add(id(d))
            if d.eng == "pe" and eng == "pe":
                continue
            if d.eng in DMAQ:
                if d.dma_i + KQ <= self.ndma[d.eng] - (1 if eng == d.eng else 0) and eng == d.eng:
                    pass
                d.sig = True
                op.deps.append(d)
            else:
                if d.eng not in best or best[d.eng].idx < d.idx:
                    best[d.eng] = d
        for d in best.values():
            d.sig = True
            op.deps.append(d)
        self.ops[eng].append(op)
        for b in writes:
            if b is not None:
                b.w = op
                b.r = []
        for b in reads:
            if b is not None and b not in writes:
                if eng in DMAQ:
                    b.r.append(op)
                else:
                    b.r = [r for r in b.r if r.eng != eng]
                    b.r.append(op)
        return op

    def barrier(self):
        lasts = []
        for e in ("pe", "act", "dve"):
            if self.ops[e]:
                lasts.append(self.ops[e][-1])
        for q in DMAQ:
            lasts.extend(self.dma_ops[q][-KQ:])
        for e in ENGS:
            op = Op(e, None)
            for d in lasts:
                if d.eng == e and e in ("pe",):
                    continue
                d.sig = True
                op.deps.append(d)
            self.ops[e].append(op)
        for b in self.bufs:
            b.w, b.r = None, []

    def mm(self, out, lhsT, rhs, start, stop):
        rd = [lhsT.buf, rhs.buf]
        self.add("pe", lambda e, o=out.ap, l=lhsT.ap, r=rhs.ap, s=start, p=stop: e.matmul(o, l, r, start=s, stop=p),
                 reads=rd, writes=[out.buf])

    def tr(self, out, in_, ident):
        self.add("pe", lambda e, o=out.ap, i=in_.ap, d=ident.ap: e.transpose(o, i, d),
                 reads=[in_.buf, ident.buf], writes=[out.buf])

    def act(self, out, in_, func, bias=None, scale=None):
        rd = [in_.buf]
        kw = {}
        if bias is not None:
            if isinstance(bias, V):
                rd.append(bias.buf); kw["bias"] = bias.ap
            else:
                kw["bias"] = bias
        if scale is not None:
            if isinstance(scale, V):
                rd.append(scale.buf); kw["scale"] = scale.ap
            else:
                kw["scale"] = scale
        self.add("act", lambda e, o=out.ap, i=in_.ap, f=func, kw=kw: e.activation(o, i, f, **kw),
                 reads=rd, writes=[out.buf])

    def tt(self, out, a, b, op, eng="dve"):
        self.add(eng, lambda e, o=out.ap, a=a.ap, b=b.ap, op=op: e.tensor_tensor(o, a, b, op),
                 reads=[a.buf, b.buf], writes=[out.buf])

    def ts(self, out, a, s1, s2, op0, op1=None, eng="dve"):
        rd = [a.buf]
        s1a = s1
        if isinstance(s1, V):
            rd.append(s1.buf); s1a = s1.ap
        s2a = s2
        if isinstance(s2, V):
            rd.append(s2.buf); s2a = s2.ap
        if op1 is None:
            self.add(eng, lambda e, o=out.ap, a=a.ap, s1a=s1a, op0=op0: e.tensor_scalar(o, a, s1a, None, op0),
                     reads=rd, writes=[out.buf])
        else:
            self.add(eng, lambda e, o=out.ap, a=a.ap, s1a=s1a, s2a=s2a, op0=op0, op1=op1:
                     e.tensor_scalar(o, a, s1a, s2a, op0, op1), reads=rd, writes=[out.buf])

    def stt(self, out, in0, scalar, in1, op0, op1, eng="dve"):
        rd = [in0.buf, in1.buf]
        sa = scalar
        if isinstance(scalar, V):
            rd.append(scalar.buf); sa = scalar.ap
        self.add(eng, lambda e, o=out.ap, a=in0.ap, sa=sa, b=in1.ap, op0=op0, op1=op1:
                 e.scalar_tensor_tensor(o, a, sa, b, op0, op1), reads=rd, writes=[out.buf])

    def copy(self, out, in_, eng="dve"):
        self.add(eng, lambda e, o=out.ap, i=in_.ap: e.tensor_copy(o, i), reads=[in_.buf], writes=[out.buf])

    def recip(self, out, in_):
        self.add("dve", lambda e, o=out.ap, i=in_.ap: e.reciprocal(o, i), reads=[in_.buf], writes=[out.buf])

    def memset(self, out, val, eng="dve"):
        self.add(eng, lambda e, o=out.ap, v=val: e.memset(o, v), writes=[out.buf])

    def reduce_sum(self, out, in_):
        self.add("dve", lambda e, o=out.ap, i=in_.ap: e.reduce_sum(o, i, AX.X), reads=[in_.buf], writes=[out.buf])

    def dma(self, q, out, in_):
        rd = [in_.buf] if isinstance(in_, V) else []
        wr = [out.buf] if isinstance(out, V) else []
        oa = out.ap if isinstance(out, V) else out
        ia = in_.ap if isinstance(in_, V) else in_
        self.add(q, lambda e, oa=oa, ia=ia: e.dma_start(out=oa, in_=ia), reads=rd, writes=wr)

    def emit(self):
        nc = self.nc
        ops = self.ops
        for e in ("pe", "act", "dve"):
            c = 0
            for op in ops[e]:
                if op.sig and op.fn is not None:
                    c += 1
                    op.cnt = c
        import contextlib
        with contextlib.ExitStack() as es:
            csem = {e: es.enter_context(nc.semaphore("s_" + e)) for e in ("pe", "act", "dve")}
            qsem = {q: [es.enter_context(nc.semaphore("q_%s%d" % (q, i))) for i in range(KQ)] for q in DMAQ}
            block = es.enter_context(nc.Block())

            def resolve(d):
                if d.eng in DMAQ:
                    return qsem[d.eng][d.dma_i % KQ], 16 * (d.dma_i // KQ + 1)
                return csem[d.eng], d.cnt

            def run(ename, e):
                waited = {}
                for op in ops[ename]:
                    for d in op.deps:
                        sem, val = resolve(d)
                        k = id(sem)
                        if waited.get(k, 0) >= val:
                            continue
                        waited[k] = val
                        e.wait_ge(sem, val)
                    if op.fn is None:
                        continue
                    ins = op.fn(e)
                    if ename in DMAQ:
                        ins.then_inc(qsem[ename][op.dma_i % KQ], 16)
                    elif op.sig:
                        ins.then_inc(csem[ename], 1)

            @block.tensor
            def _(e):
                run("pe", e)

            @block.scalar
            def _(e):
                run("act", e)

            @block.vector
            def _(e):
                run("dve", e)

            @block.gpsimd
            def _(e):
                run("pool", e)

            @block.sync
            def _(e):
                run("sp", e)


def _rope_tables():
    t = np.arange(SS)
    row, col = t // 64, t % 64
    inv = (10000.0 ** (-np.arange(16, dtype=np.float32) / 16)).astype(np.float32)
    cos = np.zeros((64, SS), np.float32); sin = np.zeros((64, SS), np.float32)
    R = np.zeros((64, 64), np.float32)
    for d in range(64):
        pos = row if d < 32 else col
        fi = (d % 32) % 16
        ang = (pos.astype(np.float32) * inv[fi]).astype(np.float32)
        cos[d] = np.cos(ang); sin[d] = np.sin(ang)
        if (d % 32) < 16:
            R[d, d + 16] = -1.0
        else:
            R[d, d - 16] = 1.0
    cos2 = np.concatenate([cos, cos], 0); sin2 = np.concatenate([sin, sin], 0)
    R2 = np.zeros((128, 128), np.float32); R2[:64, :64] = R; R2[64:, 64:] = R
    return cos2, sin2, np.ascontiguousarray(R2.T)


def _dft(n):
    k = np.arange(n, dtype=np.int64)
    ang = 2.0 * np.pi * ((k[:, None] * k[None, :]) % n).astype(np.float64) / n
    s = 1.0 / math.sqrt(n)
    return (np.cos(ang) * s).astype(np.float32), (np.sin(ang) * s).astype(np.float32)


def _invcnt(S):
    t = np.arange(S)
    out = np.zeros((4, S), np.float32)
    for g, w in enumerate((2, 4, 8, 16)):
        lo = np.clip(t - w // 2, 0, S); hi = np.clip(t + w - w // 2, 0, S)
        out[g] = 1.0 / (hi - lo).astype(np.float32)
    return out


_CONST_CACHE = {}


def _consts():
    if _CONST_CACHE:
        return _CONST_CACHE
    cos2, sin2, R2T = _rope_tables()
    c4096, s4096 = _dft(SS)
    c256, s256 = _dft(SP)
    c128, s128 = _dft(128)
    _CONST_CACHE.update(dict(
        k_ident=np.eye(128, dtype=np.float32), k_cos=cos2, k_sin=sin2, k_r2t=R2T,
        k_c4096=c4096, k_s4096=s4096,
        k_cs256=np.ascontiguousarray(np.stack([c256, s256], 0)),
        k_ccm=np.ascontiguousarray(np.concatenate([c128, -s128], 1)),
        k_inv_s=np.ascontiguousarray(np.broadcast_to(_invcnt(SS)[None], (128, 4, SS))),
        k_inv_p=np.ascontiguousarray(np.broadcast_to(_invcnt(SP)[None], (128, 4, SP))),
    ))
    return _CONST_CACHE


def build(NL=4, tiles=tuple(range(9)), lam_inits=None, lmap=None):
    nc = bass.Bass("TRN2", target_bir_lowering=False)
    K = Kern(nc)
    has_s = any(t < 8 for t in tiles)
    has_p = 8 in tiles
    s_tiles = [t for t in tiles if t < 8]

    def din(name, shape, dt=F32):
        return nc.dram_tensor(name, list(shape), dt, kind="ExternalInput").ap()

    def dout(name, shape, dt=F32):
        return nc.dram_tensor(name, list(shape), dt, kind="ExternalOutput").ap()

    def dscr(name, shape, dt):
        return nc.dram_tensor(name, list(shape), dt, kind="Internal").ap()

    xs_in = din("xs", [SS, D]); xp_in = din("xp", [2 * SP, D])
    c_ckv = din("c_ckv", [2, PAST, 512]); c_kr = din("c_kr", [2, PAST, 64])
    c_dk = din("c_dk", [2, PAST, 1536]); c_dv = din("c_dv", [2, PAST, 1536])
    cond_fm = din("cond_fm", [128, KC, 2])
    bmod_fm = din("bmod_fm", [128, 4, 144]); gnorm_fm = din("gnorm_fm", [128, 4, 3, KC])
    gfin_fm = din("gfin_fm", [128, KC]); gq_fm = din("gq_fm", [128, 2, 4]); gkv_fm = din("gkv_fm", [128, 2, 4])
    gsub_fm = din("gsub_fm", [128, 2]); pscale_fm = din("pscale_fm", [128, 2, 4]); lamqk_b = din("lamqk_b", [128, 2, 256])
    w_mod = [din("w_mod%d" % l, [D, NMOD * D]) for l in range(4)]
    w_gate = {(l, f): din("w_gate%d%d" % (l, f), [D, DFF]) for l in range(4) for f in range(2)}
    w_up = {(l, f): din("w_up%d%d" % (l, f), [D, DFF]) for l in range(4) for f in range(2)}
    w_down = {(l, f): din("w_down%d%d" % (l, f), [DFF, D]) for l in range(4) for f in range(2)}
    w_in_a = din("w_in_a", [2, D, 1600]); w_q_up = din("w_q_up", [2, 512, 2304]); w_kv_up = din("w_kv_up", [2, 512, 3072])
    w_fnet = din("w_fnet", [2, 4, 128, 128]); w_o_a = din("w_o_a", [2, D, D])
    w_in_d = din("w_in_d", [2, D, 5120]); w_pool = din("w_pool", [2, 4, 128, 128]); w_o_d = din("w_o_d", [2, D, D])
    k_ident = din("k_ident", [128, 128]); k_cos = din("k_cos", [128, SS]); k_sin = din("k_sin", [128, SS])
    k_r2t = din("k_r2t", [128, 128]); k_c4096 = din("k_c4096", [SS, SS]); k_s4096 = din("k_s4096", [SS, SS])
    k_cs256 = din("k_cs256", [2, SP, SP]); k_ccm = din("k_ccm", [128, 256])
    k_inv_s = din("k_inv_s", [128, 4, SS]); k_inv_p = din("k_inv_p", [128, 4, SP])
    y_p = dout("y_p", [2 * SP, D]); y_s = dout("y_s", [SS, D])
    o_ckv = dout("o_ckv", [2, 2, SP, 512]); o_kr = dout("o_kr", [2, 2, SP, 64])
    o_dk = dout("o_dk", [2, 2, SP, 1536]); o_dv = dout("o_dv", [2, 2, SP, 1536])
    x_scr = dscr("x_scr", [D, NTOK], F32); mix_scr = dscr("mix_scr", [D, NTOK], BF16)
    qn_scr = dscr("qn_scr", [1536, NTOK], BF16); qr_scr = dscr("qr_scr", [768, NTOK], BF16)
    ckv_scr = dscr("ckv_scr", [512, NTOK], BF16); kr_scr = dscr("kr_scr", [64, NTOK], BF16)
    fz_scr = dscr("fz_scr", [512, NTOK], BF16); pz_scr = dscr("pz_scr", [512, NTOK], F32)
    dq_scr = dscr("dq_scr", [1536, NTOK], BF16); dk_scr = dscr("dk_scr", [1536, NTOK], BF16)
    dv_scr = dscr("dv_scr", [NTOK, 1536], BF16)

    def fm(ap2):
        return ap2.rearrange("(c p) n -> p c n", p=128)

    ident = K.sb("ident", [128, 128], F32); r2t = K.sb("r2t", [128, 128], F32)
    onesb = K.sb("onesb", [128, 128], BF16)
    ones_d = K.sb("ones_d", [128, 128], BF16)
    ones_5 = K.sb("ones_5", [128, 128], BF16)
    ones_1 = K.sb("ones_1", [128, 128], BF16)
    epsb = K.sb("epsb", [128, 1], F32)
    modsb = K.sb("modsb", [128, 4, NMOD, KC, 2], F32)
    geff = K.sb("geff", [128, 4, 3, KC, 2], F32)
    gate = K.sb("gate", [128, 4, 3, KC, 2], F32)
    gnorm = K.sb("gnorm", [128, 4, 3, KC], F32); gfin = K.sb("gfin", [128, KC], F32)
    gq = K.sb("gq", [128, 2, 4], F32); gkv = K.sb("gkv", [128, 2, 4], F32)
    gsub = K.sb("gsub", [128, 2], F32); pscale = K.sb("pscale", [128, 2, 4], F32)
    lamsb = K.sb("lamsb", [128, 2, 8], F32)
    A32 = K.sb("A32", [128, 13312], F32)
    A16 = K.sb("A16", [128, 43008], BF16)
    wring = [K.sb("wr%d" % i, [128, WSZ], BF16) for i in range(NWS)]
    PS = [K.psb("ps%d" % i, [128, 512], F32) for i in range(8)]
    st = {"ps_a": 0, "ps_b": 0, "wr": 0}

    def psA():
        st["ps_a"] = (st["ps_a"] + 1) % 4
        return PS[st["ps_a"]]

    def psB():
        st["ps_b"] = (st["ps_b"] + 1) % 4
        return PS[4 + st["ps_b"]]

    def wload(src, nk, ncols):
        assert nk * ncols <= WSZ
        st["wr"] = (st["wr"] + 1) % NWS
        slot = wring[st["wr"]]
        P = src.shape[0]
        v = V(slot, slot.ap[0:P, 0:nk * ncols].rearrange("p (k n) -> p k n", k=nk))
        K.dma("pool", v, src)
        return v

    def a32(off, shape):
        n = int(np.prod(shape[1:]))
        assert off + n <= 13312
        v = V(A32, A32.ap[0:shape[0], off:off + n])
        if len(shape) == 3:
            v = v.re("p (a b) -> p a b", a=shape[1])
        return v

    def a16(off, shape):
        n = int(np.prod(shape[1:]))
        assert off + n <= 43008, (off, n)
        v = V(A16, A16.ap[0:shape[0], off:off + n])
        if len(shape) == 3:
            v = v.re("p (a b) -> p a b", a=shape[1])
        elif len(shape) == 4:
            v = v.re("p (a b c) -> p a b c", a=shape[1], b=shape[2])
        return v

    def sub(parent_v, name):
        b = Buf(name, parent_v.ap)
        K.bufs.append(b)
        return b

    K.dma("sp", ident[:], k_ident); K.dma("sp", r2t[:], k_r2t)
    K.dma("sp", gnorm[:], gnorm_fm); K.dma("sp", gfin[:], gfin_fm); K.dma("sp", gq[:], gq_fm)
    K.dma("sp", gkv[:], gkv_fm); K.dma("sp", gsub[:], gsub_fm); K.dma("sp", pscale[:], pscale_fm)
    K.memset(onesb[:], 1.0); K.memset(ones_d[:], 1.0 / 2048); K.memset(ones_5[:], 1.0 / 512)
    K.memset(ones_1[:], 1.0 / 128); K.memset(epsb[:], EPS)
    condf = sub(a32(0, [128, 32]), "condf"); bmod = sub(a32(64, [128, 576]), "bmod")
    scond = sub(a16(0, [128, 32]), "scond")
    lamq = sub(a32(1024, [128, 512]), "lamq"); lamt = sub(a32(2048, [128, 256]), "lamt")
    K.dma("sp", condf[:], cond_fm.rearrange("p k c -> p (k c)"))
    K.dma("sp", bmod[:], bmod_fm.rearrange("p l m -> p (l m)"))
    K.dma("sp", lamq[:], lamqk_b.rearrange("p l m -> p (l m)"))
    K.act(scond[:], condf[:], AF.Silu)
    scv = scond[:].re("p (k c) -> p k c", c=2)
    import os as _os
    for l in range(int(_os.environ.get('NMOD_DBG', NL))):
        pm = psB()
        wv = fm(w_mod[l])
        for mp in range(72):
            w = wload(wv[:, :, mp * 256:(mp + 1) * 256], KC, 256)
            for jj in range(2):
                m = mp * 2 + jj
                for kc in range(KC):
                    K.mm(pm[:, 2 * m:2 * m + 2], w[:, kc, jj * 128:(jj + 1) * 128], scv[:, kc, :], kc == 0, kc == KC - 1)
        pmv = pm[:, 0:288].re("p (m c) -> p m c", c=2)
        bv = bmod[:, l * 144:(l + 1) * 144]
        mv = modsb[:, l].re("p i k c -> p (i k) c")
        for c in range(2):
            K.tt(mv[:, :, c], pmv[:, :, c], bv, ALU.add)
        for n in range(3):
            for c in range(2):
                K.stt(geff[:, l, n, :, c], modsb[:, l, 3 * n + 1, :, c], 1.0, gnorm[:, l, n, :], ALU.add, ALU.mult)
                K.ts(gate[:, l, n, :, c], modsb[:, l, 3 * n + 2, :, c], (1.0 if n == 1 else 0.5), None, ALU.mult)
    if lam_inits is None:
        lam_inits = [0.8 - 0.6 * math.exp(-0.3 * l) for l in range(4)]
    for i in range(2):
        lq = lamq[:, i * 256:(i + 1) * 256]
        K.tt(lamt[:, 0:64], lq[:, 0:64], lq[:, 64:128], ALU.mult)
        K.tt(lamt[:, 64:128], lq[:, 128:192], lq[:, 192:256], ALU.mult)
        K.reduce_sum(lamsb[:, i, 1:2], lamt[:, 0:64]); K.reduce_sum(lamsb[:, i, 2:3], lamt[:, 64:128])
        K.act(lamsb[:, i, 3:5], lamsb[:, i, 1:3], AF.Exp)
        K.tt(lamsb[:, i, 5:6], lamsb[:, i, 4:5], lamsb[:, i, 3:4], ALU.subtract)
        K.ts(lamsb[:, i, 0:1], lamsb[:, i, 5:6], -lam_inits[2 * i + 1], None, ALU.add)
    K.barrier()

    xb = sub(a32(0, [128, KC, T]), "xb")
    rstd = sub(a32(8192, [128, T]), "rstd"); tmpA = sub(a32(8704, [128, T]), "tmpA")
    tmpB = sub(a32(9216, [128, T]), "tmpB"); sg0 = sub(a32(9728, [128, T]), "sg0"); sg1 = sub(a32(10240, [128, T]), "sg1")
    lat = sub(a32(10752, [128, 4, T]), "lat")
    stg32 = sub(a32(12800, [128, 512]), "stg32")
    hb = sub(a16(0, [128, KC, T]), "hb")
    hid = sub(a16(8192, [128, JC, T]), "hid")
    sq0 = sub(a16(30720, [128, T]), "sq0"); sq1 = sub(a16(31232, [128, T]), "sq1")
    stgA = hid[:, 0:12, :]; stgB = hid[:, 12:24, :]; qnb = hid[:, 24:28, :]
    cosT = K.sb("cosT", [128, T], F32); sinT = K.sb("sinT", [128, T], F32)
    wsm = K.sb("wsm", [128, 4, 128], BF16)
    sgs = [sg0, sg1]; sqs = [sq0, sq1]
    cnt = {"sg": 0, "sq": 0}

    def rms_stats(src, nch, ones_m, out_rstd):
        pss = psB()
        for c in range(nch):
            cnt["sq"] ^= 1
            sq = sqs[cnt["sq"]]
            K.tt(sq[:], src[:, c, :], src[:, c, :], ALU.mult)
            K.mm(pss[:], ones_m[:], sq[:], c == 0, c == nch - 1)
        K.act(tmpA[:], pss[:], AF.Sqrt, bias=epsb[:, 0:1])
        K.recip(out_rstd, tmpA[:])

    def norm_mod(l, n, c):
        rms_stats(xb[:], KC, ones_d, rstd[:])
        for kc in range(KC):
            K.tt(tmpB[:], xb[:, kc, :], rstd[:], ALU.mult)
            K.act(hb[:, kc, :], tmpB[:], AF.Identity, bias=modsb[:, l, 3 * n, kc, c:c + 1], scale=geff[:, l, n, kc, c:c + 1])

    def ffn(l, f, c):
        n = 0 if f == 0 else 2
        norm_mod(l, n, c)
        wg_v = fm(w_gate[l, f]); wu_v = fm(w_up[l, f]); wd_v = fm(w_down[l, f])
        for jp in range(JC // 2):
            wg = wload(wg_v[:, :, jp * 256:(jp + 1) * 256], KC, 256)
            wu = wload(wu_v[:, :, jp * 256:(jp + 1) * 256], KC, 256)
            for jj in range(2):
                j = jp * 2 + jj
                pg = psA(); pu = psA()
                for kc in range(KC):
                    K.mm(pg[:], wg[:, kc, jj * 128:(jj + 1) * 128], hb[:, kc, :], kc == 0, kc == KC - 1)
                for kc in range(KC):
                    K.mm(pu[:], wu[:, kc, jj * 128:(jj + 1) * 128], hb[:, kc, :], kc == 0, kc == KC - 1)
                cnt["sg"] ^= 1
                sg = sgs[cnt["sg"]]
                K.act(sg[:], pg[:], AF.Silu)
                K.tt(hid[:, j, :], sg[:], pu[:], ALU.mult)
        for mp in range(8):
            p = [psB(), psB()]
            for g in range(4):
                wd = wload(wd_v[:, g * 11:(g + 1) * 11, mp * 256:(mp + 1) * 256], 11, 256)
                for mi in range(2):
                    for jj in range(11):
                        j = g * 11 + jj
                        K.mm(p[mi][:], wd[:, jj, mi * 128:(mi + 1) * 128], hid[:, j, :], j == 0, j == JC - 1)
            for mi in range(2):
                m = mp * 2 + mi
                K.stt(xb[:, m, :], p[mi][:], gate[:, l, n, m, c:c + 1], xb[:, m, :], ALU.mult, ALU.add)

    xs_v = fm(x_scr)

    def load_x_input(t):
        src = xs_in if t < 8 else xp_in
        r0 = t * T if t < 8 else 0
        sv = lat[:].re("p a b -> p (a b)")
        for s in range(4):
            K.dma("sp", sv, src[r0 + s * 128:r0 + (s + 1) * 128, :])
            for g4 in range(4):
                pt = psA()
                for q in range(4):
                    kc = g4 * 4 + q
                    K.tr(pt[:, q * 128:(q + 1) * 128], sv[:, kc * 128:(kc + 1) * 128], ident[:])
                K.copy(xb[:, g4 * 4:(g4 + 1) * 4, s * 128:(s + 1) * 128], pt[:].re("p (q n) -> p q n", q=4))

    def load_x(t):
        K.dma("sp", xb[:], xs_v[:, :, t * T:(t + 1) * T])

    def store_x(t):
        K.dma("sp", xs_v[:, :, t * T:(t + 1) * T], xb[:])

    def tok_major_out(srcv, nch, dst_rows):
        P = srcv.ap.shape[0]
        for s in range(4):
            sv = V(stg32, stg32.ap[:, 0:512])
            done = 0
            while done < nch:
                nb = min(512 // P, nch - done)
                pt = psA()
                for q in range(nb):
                    K.tr(pt[:, q * P:(q + 1) * P], srcv[:, done + q, s * 128:(s + 1) * 128], ident[0:P, 0:P])
                K.copy(sv[:, 0:nb * P], pt[:, 0:nb * P])
                K.dma("sp", dst_rows(s)[:, done * P:(done + nb) * P], sv[:, 0:nb * P])
                done += nb

    def final_out(t, c):
        rms_stats(xb[:], KC, ones_d, rstd[:])
        dst = y_s if t < 8 else y_p
        r0 = t * T if t < 8 else 0
        for s in range(4):
            for g4 in range(4):
                pt = psA()
                for q in range(4):
                    kc = g4 * 4 + q
                    cnt["sg"] ^= 1
                    tb = sgs[cnt["sg"]]
                    K.tt(tmpB[:, 0:128], xb[:, kc, s * 128:(s + 1) * 128], rstd[:, s * 128:(s + 1) * 128], ALU.mult)
                    K.act(tb[:, 0:128], tmpB[:, 0:128], AF.Copy, scale=gfin[:, kc:kc + 1])
                    K.tr(pt[:, q * 128:(q + 1) * 128], tb[:, 0:128], ident[:])
                sv = V(stg32, stg32.ap[:, 0:512])
                K.copy(sv, pt[:])
                K.dma("sp", dst[r0 + s * 128:r0 + (s + 1) * 128, g4 * 512:(g4 + 1) * 512], sv)

    def rope_tables(t):
        K.dma("sp", cosT[:], k_cos[:, t * T:(t + 1) * T])
        K.dma("sp", sinT[:], k_sin[:, t * T:(t + 1) * T])

    def rope(dst_bf, src32, P, t):
        pr = psB()
        K.mm(pr[0:P, :], r2t[0:P, 0:P], src32, True, True)
        K.tt(tmpA[0:P, :], src32, cosT[0:P, :], ALU.mult)
        K.tt(tmpB[0:P, :], pr[0:P, :], sinT[0:P, :], ALU.mult)
        K.tt(dst_bf, tmpA[0:P, :], tmpB[0:P, :], ALU.add)

    def p1_mla(i, t, c):
        is_p = (t == 8)
        tok = slice(t * T, (t + 1) * T)
        wv = fm(w_in_a[i])
        wq = fm(w_q_up[i])
        if not is_p:
            rope_tables(t)

        def latent(base, g_fm, ones_m):
            for gp in range(2):
                w = wload(wv[:, :, base + gp * 256:base + (gp + 1) * 256], KC, 256)
                for jj in range(2):
                    p = psA()
                    for kc in range(KC):
                        K.mm(p[:], w[:, kc, jj * 128:(jj + 1) * 128], hb[:, kc, :], kc == 0, kc == KC - 1)
                    K.act(lat[:, gp * 2 + jj, :], p[:], AF.Copy)
            rms_stats(lat[:], 4, ones_5, rstd[:])
            for cc in range(4):
                K.tt(tmpB[:], lat[:, cc, :], rstd[:], ALU.mult)
                K.act(lat[:, cc, :], tmpB[:], AF.Copy, scale=g_fm[:, i, cc:cc + 1])

        latent(0, gq, ones_5)
        for cc in range(4):
            K.copy(qnb[:, cc, :], lat[:, cc, :])
        for hg in range(3):
            w = wload(wq[:, :, hg * 768:(hg + 1) * 768], 4, 768)
            for hh in range(4):
                h = hg * 4 + hh
                p = psA()
                for cc in range(4):
                    K.mm(p[:], w[:, cc, hh * 192:hh * 192 + 128], qnb[:, cc, :], cc == 0, cc == 3)
                K.act(stgA[:, h, :], p[:], AF.Copy)
                p2 = psA()
                for cc in range(4):
                    K.mm(p2[0:64, :], w[:, cc, hh * 192 + 128:hh * 192 + 192], qnb[:, cc, :], cc == 0, cc == 3)
                if is_p:
                    K.act(stgB[0:64, h, :], p2[0:64, :], AF.Copy)
                else:
                    cnt["sg"] ^= 1
                    q32 = sgs[cnt["sg"]]
                    K.act(q32[0:64, :], p2[0:64, :], AF.Copy)
                    rope(stgB[0:64, h, :], q32[0:64, :], 64, t)
        K.dma("sp", fm(qn_scr)[:, :, tok], stgA[:, 0:12, :])
        K.dma("sp", qr_scr.rearrange("(h p) n -> p h n", p=64)[:, :, tok], stgB[0:64, 0:12, :])
        latent(512, gkv, ones_5)
        for cc in range(4):
            K.copy(qnb[:, cc, :], lat[:, cc, :])
        K.dma("sp", fm(ckv_scr)[:, :, tok], qnb[:, 0:4, :])
        if is_p:
            tok_major_out(lat[:], 4, lambda s: o_ckv[s // 2, i, (s % 2) * 128:(s % 2 + 1) * 128, :])
        w = wload(wv[:, :, 1024:1088], KC, 64)
        p = psA()
        for kc in range(KC):
            K.mm(p[0:64, :], w[:, kc, :], hb[:, kc, :], kc == 0, kc == KC - 1)
        K.act(lat[0:64, 0, :], p[0:64, :], AF.Copy)
        if is_p:
            K.copy(stgA[0:64, 0, :], lat[0:64, 0, :])
            tok_major_out(lat[0:64, 0:1, :], 1, lambda s: o_kr[s // 2, i, (s % 2) * 128:(s % 2 + 1) * 128, :])
        else:
            rope(stgA[0:64, 0, :], lat[0:64, 0, :], 64, t)
        K.dma("sp", kr_scr[:, tok], stgA[0:64, 0, :])
        for gp in range(2):
            w = wload(wv[:, :, 1088 + gp * 256:1088 + (gp + 1) * 256], KC, 256)
            for jj in range(2):
                p = psA()
                for kc in range(KC):
                    K.mm(p[:], w[:, kc, jj * 128:(jj + 1) * 128], hb[:, kc, :], kc == 0, kc == KC - 1)
                K.act(stgB[:, gp * 2 + jj, :], p[:], AF.Copy)
        K.dma("sp", fm(fz_scr)[:, :, tok], stgB[:, 0:4, :])

    def p1_diff(i, t, c):
        is_p = (t == 8)
        tok = slice(t * T, (t + 1) * T)
        wv = fm(w_in_d[i])
        if not is_p:
            rope_tables(t)
        for gp in range(2):
            w = wload(wv[:, :, gp * 256:(gp + 1) * 256], KC, 256)
            for jj in range(2):
                p = psA()
                for kc in range(KC):
                    K.mm(p[:], w[:, kc, jj * 128:(jj + 1) * 128], hb[:, kc, :], kc == 0, kc == KC - 1)
                K.act(lat[:, gp * 2 + jj, :], p[:], AF.Copy)
        K.dma("sp", fm(pz_scr)[:, :, tok], lat[:, 0:4, :])
        for which, base, stg, scr in (("q", 512, stgA, dq_scr), ("k", 2048, stgB, dk_scr)):
            for gp in range(6):
                w = wload(wv[:, :, base + gp * 256:base + (gp + 1) * 256], KC, 256)
                for jj in range(2):
                    h = gp * 2 + jj
                    p = psA()
                    for kc in range(KC):
                        K.mm(p[:], w[:, kc, jj * 128:(jj + 1) * 128], hb[:, kc, :], kc == 0, kc == KC - 1)
                    if is_p:
                        K.act(stg[:, h, :], p[:], AF.Copy)
                        if which == "k":
                            K.act(lat[:, h % 4, :], p[:], AF.Copy)
                            if h % 4 == 3:
                                h0 = h - 3
                                tok_major_out(lat[:], 4, lambda s, h0=h0: o_dk[s // 2, i, (s % 2) * 128:(s % 2 + 1) * 128, h0 * 128:(h0 + 4) * 128])
                    else:
                        cnt["sg"] ^= 1
                        q32 = sgs[cnt["sg"]]
                        K.act(q32[:], p[:], AF.Copy)
                        rope(stg[:, h, :], q32[:], 128, t)
            K.dma("sp", fm(scr)[:, :, tok], stg[:, 0:12, :])
        for gp in range(0 if "vpath" not in _os.environ.get("SKIP_DBG", "") else 6, 6):
            w = wload(wv[:, :, 3584 + gp * 256:3584 + (gp + 1) * 256], KC, 256)
            for s in range(4):
                p = psA()
                for kc in range(KC):
                    K.mm(p[:, 0:256], hb[:, kc, s * 128:(s + 1) * 128], w[:, kc, :], kc == 0, kc == KC - 1)
                K.act(stgA[:, s, 0:256], p[:, 0:256], AF.Copy)
                if is_p:
                    sv = V(stg32, stg32.ap[:, 0:256])
                    K.act(sv, p[:, 0:256], AF.Copy)
                    K.dma("sp", o_dv[s // 2, i, (s % 2) * 128:(s % 2 + 1) * 128, gp * 256:(gp + 1) * 256], sv)
            K.dma("sp", dv_scr[t * T:(t + 1) * T, gp * 256:(gp + 1) * 256].rearrange("(s p) n -> p s n", p=128), stgA[:, 0:4, 0:256])

    def p3(l, t, c):
        i = l // 2
        wo = fm((w_o_a if l % 2 == 0 else w_o_d)[i])
        mixt = hid[:, 0:KC, :]
        K.dma("sp", mixt, fm(mix_scr)[:, :, t * T:(t + 1) * T])
        for mp in range(8):
            w = wload(wo[:, :, mp * 256:(mp + 1) * 256], KC, 256)
            for mi in range(2):
                m = mp * 2 + mi
                p = psB()
                for kc in range(KC):
                    K.mm(p[:], w[:, kc, mi * 128:(mi + 1) * 128], mixt[:, kc, :], kc == 0, kc == KC - 1)
                K.stt(xb[:, m, :], p[:], gate[:, l, 1, m, c:c + 1], xb[:, m, :], ALU.mult, ALU.add)

    def seqs_for(tiles_):
        out = []
        if has_s:
            out.append(dict(tok0=0, S=SS, cache=True, qt=T, sample=True))
        if has_p:
            out.append(dict(tok0=SS, S=SP, cache=False, qt=SP, sample=False))
            out.append(dict(tok0=SS + SP, S=SP, cache=False, qt=SP, sample=False))
        return out

    def softmax_av(nq, kts, score_fn, v_fn, pO, pS, ptile, scale):
        nk = len(kts)
        for ki, kt in enumerate(kts):
            psc = psA()
            score_fn(psc, kt)
            cnt["pt"] = (cnt.get("pt", 0) + 1) % 3
            pt = ptile[cnt["pt"]]
            K.act(pt[:, 0:nq], psc[:, 0:nq], AF.Exp, scale=scale)
            K.mm(pO[:, 0:nq], v_fn(kt), pt[:, 0:nq], ki == 0, ki == nk - 1)
            K.mm(pS[:, 0:nq], onesb[:], pt[:, 0:nq], ki == 0, ki == nk - 1)

    def p2_mla(i):
        ckvT = sub(a16(0, [128, 4, NTOK]), "ckvT")
        krT = sub(a16(18432, [128, NTOK]), "krT")
        knT = sub(a16(23040, [128, NTOK]), "knT")
        Vh = sub(a16(27648, [128, 36, 128]), "Vh")
        qnh = sub(a16(32256, [128, SS]), "qnh")
        qrh = sub(a16(36352, [128, SS]), "qrh")
        pts = [sub(a16(40448 + k * 512, [128, 512]), "pt%d" % k) for k in range(3)]
        osb = sub(a16(41984, [128, 512]), "osb")
        cst = sub(a32(0, [128, 4, 512]), "cst")
        rsum = sub(a32(2048, [128, 512]), "rsum")
        wkv = fm(w_kv_up[i])
        for sq_ in seqs_for(tiles):
            tok0, S, qt = sq_["tok0"], sq_["S"], sq_["qt"]
            nkeys = S + (PAST if sq_["cache"] else 0)
            nkt = nkeys // 128
            K.dma("sp", ckvT[:, :, 0:S], fm(ckv_scr)[:, :, tok0:tok0 + S])
            K.dma("sp", krT[0:64, 0:S], kr_scr[:, tok0:tok0 + S])
            if sq_["cache"]:
                K.dma("sp", cst[:], c_ckv[i].rearrange("(s p) f -> p s f", p=128))
                for s in range(4):
                    ptp = psA()
                    for cc in range(4):
                        K.tr(ptp[:, cc * 128:(cc + 1) * 128], cst[:, s, cc * 128:(cc + 1) * 128], ident[:])
                    K.copy(ckvT[:, :, S + s * 128:S + (s + 1) * 128], ptp[:].re("p (c n) -> p c n", c=4))
                K.dma("sp", cst[:, :, 0:64], c_kr[i].rearrange("(s p) f -> p s f", p=128))
                ptp = psA()
                for s in range(4):
                    K.tr(ptp[0:64, s * 128:(s + 1) * 128], cst[:, s, 0:64], ident[:])
                K.copy(krT[0:64, S:S + 512], ptp[0:64, :])
            for h in range(12):
                w = wload(wkv[:, :, h * 256:(h + 1) * 256], 4, 256)
                for k0 in range(0, nkeys, 512):
                    n = min(512, nkeys - k0)
                    p = psA()
                    for cc in range(4):
                        K.mm(p[:, 0:n], w[:, cc, 0:128], ckvT[:, cc, k0:k0 + n], cc == 0, cc == 3)
                    K.act(knT[:, k0:k0 + n], p[:, 0:n], AF.Copy)
                for k4 in range(0, nkt, 4):
                    nb = min(4, nkt - k4)
                    p = psA()
                    for q in range(nb):
                        kt = k4 + q
                        for cc in range(4):
                            K.mm(p[:, q * 128:(q + 1) * 128], ckvT[:, cc, kt * 128:(kt + 1) * 128], w[:, cc, 128:256], cc == 0, cc == 3)
                    K.copy(Vh[:, k4:k4 + nb, :], p[:, 0:nb * 128].re("p (q n) -> p q n", q=nb))
                K.dma("sp", qnh[:, 0:S], qn_scr[h * 128:(h + 1) * 128, tok0:tok0 + S])
                K.dma("sp", qrh[0:64, 0:S], qr_scr[h * 64:(h + 1) * 64, tok0:tok0 + S])
                for q0 in range(0, S, qt):
                    pO = psB(); pS = psB()

                    def score(psc, kt, q0=q0):
                        K.mm(psc[:, 0:qt], knT[:, kt * 128:(kt + 1) * 128], qnh[:, q0:q0 + qt], True, False)
                        K.mm(psc[:, 0:qt], krT[0:64, kt * 128:(kt + 1) * 128], qrh[0:64, q0:q0 + qt], False, True)

                    softmax_av(qt, list(range(nkt)), score, lambda kt: Vh[:, kt, :], pO, pS, pts, MLA_SCALE)
                    K.recip(rsum[:, 0:qt], pS[:, 0:qt])
                    K.tt(osb[:, 0:qt], pO[:, 0:qt], rsum[:, 0:qt], ALU.mult)
                    K.dma("sp", mix_scr[h * 128:(h + 1) * 128, tok0 + q0:tok0 + q0 + qt], osb[:, 0:qt])
        K.barrier()

    def p2_fourier(i):
        zT = sub(a16(0, [128, 2, SS]), "zT")
        ab = sub(a16(8192, [128, 32, 512]), "ab")
        fsb = sub(a16(24576, [128, 512]), "fsb"); osb = sub(a16(25088, [128, 512]), "osbf")
        K.dma("pool", wsm[:], w_fnet[i].rearrange("g c d -> c g d"))
        for sq_ in seqs_for(tiles):
            tok0, S = sq_["tok0"], sq_["S"]
            nsc = S // 128
            for gh in range(2):
                K.dma("sp", zT[:, :, 0:S], fm(fz_scr)[:, gh * 2:gh * 2 + 2, tok0:tok0 + S])
                for g2 in range(2):
                    for sc in range(nsc):
                        p = psA()
                        K.mm(p[:, 0:256], zT[:, g2, sc * 128:(sc + 1) * 128], ccm[:], True, True)
                        K.act(ab[:, sc, g2 * 256:(g2 + 1) * 256], p[:, 0:256], AF.Copy)
                nst = max(1, S // 512)
                stw = min(S, 512)
                for stile in range(nst):
                    pf = [psB(), psB()]
                    if sq_["sample"]:
                        tabs = (fm(k_c4096), fm(k_s4096))
                        for ti in range(2):
                            for gq_ in range(4):
                                w = wload(tabs[ti][:, gq_ * 8:(gq_ + 1) * 8, stile * 512:(stile + 1) * 512], 8, 512)
                                for g2 in range(2):
                                    for kk in range(8):
                                        sc = gq_ * 8 + kk
                                        K.mm(pf[g2][:], ab[:, sc, g2 * 256 + ti * 128:g2 * 256 + (ti + 1) * 128], w[:, kk, :],
                                             ti == 0 and sc == 0, ti == 1 and sc == 31)
                    else:
                        for ti in range(2):
                            for g2 in range(2):
                                for sc in range(2):
                                    K.mm(pf[g2][:, 0:stw], ab[:, sc, g2 * 256 + ti * 128:g2 * 256 + (ti + 1) * 128],
                                         cs256[:, ti * 2 + sc, :], ti == 0 and sc == 0, ti == 1 and sc == 1)
                    for g2 in range(2):
                        g = gh * 2 + g2
                        K.act(fsb[:, 0:stw], pf[g2][:, 0:stw], AF.Copy)
                        p = psA()
                        K.mm(p[:, 0:stw], wsm[:, g, :], fsb[:, 0:stw], True, True)
                        K.act(osb[:, 0:stw], p[:, 0:stw], AF.Copy)
                        K.dma("sp", mix_scr[(12 + g) * 128:(13 + g) * 128, tok0 + stile * 512:tok0 + stile * 512 + stw], osb[:, 0:stw])
        K.barrier()

    def p2_pool(i):
        W16 = SS + 16
        xp = sub(a32(0, [128, W16]), "xp"); sa = sub(a32(W16, [128, W16]), "sa")
        sbb = sub(a32(2 * W16, [128, W16]), "sbb")
        invf = sub(a32(3 * W16, [128, 976]), "invf")
        pooled = sub(a16(0, [128, SS]), "pooled")
        osb = sub(a16(SS, [128, 512]), "osbp")
        K.dma("pool", wsm[:], w_pool[i].rearrange("g c d -> c g d"))
        wp = wsm
        for sq_ in seqs_for(tiles):
            tok0, S = sq_["tok0"], sq_["S"]
            ktab = k_inv_s if sq_["sample"] else k_inv_p
            for g, wdw in enumerate((2, 4, 8, 16)):
                hw = wdw // 2
                K.memset(xp[:, 0:W16], 0.0)
                K.dma("sp", xp[:, hw:hw + S], pz_scr[g * 128:(g + 1) * 128, tok0:tok0 + S])
                cur = xp; L = S + wdw - 1
                step = 1
                nxt = [sa, sbb]; ni = 0
                while step < wdw:
                    L2 = L - step
                    K.tt(nxt[ni][:, 0:L2], cur[:, 0:L2], cur[:, step:step + L2], ALU.add)
                    cur = nxt[ni]; ni ^= 1; L = L2; step *= 2
                other = nxt[ni]
                for c0 in range(0, S, 976):
                    n = min(976, S - c0)
                    K.dma("sp", invf[:, 0:n], ktab[:, g, c0:c0 + n])
                    K.tt(other[:, c0:c0 + n], cur[:, c0:c0 + n], invf[:, 0:n], ALU.mult)
                K.tt(pooled[:, 0:S], other[:, 0:S], xp[:, hw:hw + S], ALU.subtract)
                for s0 in range(0, S, 512):
                    n = min(512, S - s0)
                    p = psA()
                    K.mm(p[:, 0:n], wp[:, g, :], pooled[:, s0:s0 + n], True, True)
                    K.act(osb[:, 0:n], p[:, 0:n], AF.Copy, scale=pscale[:, i, g:g + 1])
                    K.dma("sp", mix_scr[g * 128:(g + 1) * 128, tok0 + s0:tok0 + s0 + n], osb[:, 0:n])
        K.barrier()

    def p2_diff(i, lam_init):
        KT = sub(a16(0, [128, NTOK]), "KT")
        Vd = sub(a16(4608, [128, 36, 128]), "Vd")
        Qd = sub(a16(9216, [128, SS]), "Qd")
        pts = [sub(a16(13312 + k * 512, [128, 512]), "ptd%d" % k) for k in range(3)]
        osb = sub(a16(14848, [128, 512]), "osbd")
        sqd = sub(a16(15360, [128, 512]), "sqd")
        cst = sub(a32(0, [128, 4, 128]), "cstd")
        r0 = sub(a32(512, [128, 512]), "r0"); r1 = sub(a32(1024, [128, 512]), "r1")
        o0 = sub(a32(1536, [128, 512]), "o0"); o1 = sub(a32(2048, [128, 512]), "o1")
        rs = sub(a32(2560, [128, 512]), "rsd"); tq = sub(a32(3072, [128, 512]), "tq")
        gs = sub(a32(3584, [128, 2]), "gs")
        K.ts(gs[:, 0:1], gsub[:, i:i + 1], (1.0 - lam_init), None, ALU.mult)
        for sq_ in seqs_for(tiles):
            tok0, S, qt = sq_["tok0"], sq_["S"], sq_["qt"]
            nkeys = S + (PAST if sq_["cache"] else 0)
            nkt = nkeys // 128
            for h in range(12):
                K.dma("sp", KT[:, 0:S], dk_scr[h * 128:(h + 1) * 128, tok0:tok0 + S])
                K.dma("sp", Qd[:, 0:S], dq_scr[h * 128:(h + 1) * 128, tok0:tok0 + S])
                K.dma("sp", Vd[:, 0:S // 128, :], dv_scr[tok0:tok0 + S, h * 128:(h + 1) * 128].rearrange("(s p) n -> p s n", p=128))
                if sq_["cache"]:
                    K.dma("sp", cst[:], c_dk[i][:, h * 128:(h + 1) * 128].rearrange("(s p) f -> p s f", p=128))
                    ptp = psA()
                    for s in range(4):
                        K.tr(ptp[:, s * 128:(s + 1) * 128], cst[:, s, :], ident[:])
                    K.copy(KT[:, S:S + 512], ptp[:])
                    K.dma("pool", Vd[:, S // 128:S // 128 + 4, :], c_dv[i][:, h * 128:(h + 1) * 128].rearrange("(s p) f -> p s f", p=128))
                for q0 in range(0, S, qt):
                    pOs = []
                    for cidx in range(2):
                        pO = psB(); pS = psB()
                        lo = cidx * 64

                        def score(psc, kt, q0=q0, lo=lo):
                            K.mm(psc[:, 0:qt], KT[lo:lo + 64, kt * 128:(kt + 1) * 128], Qd[lo:lo + 64, q0:q0 + qt], True, True)

                        softmax_av(qt, list(range(nkt)), score, lambda kt: Vd[:, kt, :], pO, pS, pts, DIFF_SCALE)
                        pOs.append((pO, pS))
                    K.recip(r0[:, 0:qt], pOs[0][1][:, 0:qt]); K.recip(r1[:, 0:qt], pOs[1][1][:, 0:qt])
                    K.tt(o0[:, 0:qt], pOs[0][0][:, 0:qt], r0[:, 0:qt], ALU.mult)
                    K.tt(o1[:, 0:qt], pOs[1][0][:, 0:qt], r1[:, 0:qt], ALU.mult)
                    K.stt(o0[:, 0:qt], o1[:, 0:qt], lamsb[:, i, 0:1], o0[:, 0:qt], ALU.mult, ALU.add)
                    K.tt(sqd[:, 0:qt], o0[:, 0:qt], o0[:, 0:qt], ALU.mult)
                    pn = psA()
                    K.mm(pn[:, 0:qt], ones_1[:], sqd[:, 0:qt], True, True)
                    K.act(tq[:, 0:qt], pn[:, 0:qt], AF.Sqrt, bias=epsb[:, 0:1])
                    K.recip(rs[:, 0:qt], tq[:, 0:qt])
                    K.tt(o1[:, 0:qt], o0[:, 0:qt], rs[:, 0:qt], ALU.mult)
                    K.act(osb[:, 0:qt], o1[:, 0:qt], AF.Copy, scale=gs[:, 0:1])
                    K.dma("sp", mix_scr[(4 + h) * 128:(5 + h) * 128, tok0 + q0:tok0 + q0 + qt], osb[:, 0:qt])
        K.barrier()

    ccm = K.sb("ccm", [128, 256], BF16)
    cs256 = K.sb("cs256", [128, 4, 256], BF16)
    K.dma("pool", ccm[:], k_ccm)
    K.dma("pool", cs256[:], k_cs256.rearrange("t (s p) n -> p (t s) n", p=128))
    K.barrier()

    LM = list(lmap) if lmap is not None else list(range(NL))
    for li in range(NL):
        l = LM[li]
        i = l // 2
        for t in tiles:
            c = 0 if t < 8 else 1
            if li == 0:
                load_x_input(t)
            else:
                lp = LM[li - 1]
                load_x(t)
                p3(lp, t, c)
                ffn(lp, 1, c)
            ffn(l, 0, c)
            norm_mod(l, 1, c)
            if l % 2 == 0:
                p1_mla(i, t, c)
            else:
                p1_diff(i, t, c)
            store_x(t)
        K.barrier()
        _skip = _os.environ.get("SKIP_DBG", "").split(",")
        if l % 2 == 0:
            if "p2mla" not in _skip:
                p2_mla(i)
            if "p2four" not in _skip:
                p2_fourier(i)
        else:
            if "p2pool" not in _skip:
                p2_pool(i)
            if "p2diff" not in _skip:
                p2_diff(i, lam_inits[l])
    l = LM[NL - 1]
    for t in tiles:
        c = 0 if t < 8 else 1
        load_x(t)
        p3(l, t, c)
        ffn(l, 1, c)
        final_out(t, c)
    K.barrier()
    K.emit()
    return nc


def _core_inputs(inp, b, shared):
    f = np.float32
    cond = np.stack([inp["c"][b], inp["c_ctx"]], 0).astype(f)
    m = dict(shared)
    m.update(
        xs=np.ascontiguousarray(inp["x_sample"][b]),
        xp=np.ascontiguousarray(inp["x_prompt"][2 * b:2 * b + 2].reshape(2 * SP, D)),
        c_ckv=np.ascontiguousarray(inp["cache_mla_ckv"][b]),
        c_kr=np.ascontiguousarray(inp["cache_mla_krope"][b]),
        c_dk=np.ascontiguousarray(inp["cache_diff_k"][b].reshape(2, PAST, 1536)),
        c_dv=np.ascontiguousarray(inp["cache_diff_v"][b].reshape(2, PAST, 1536)),
        cond_fm=np.ascontiguousarray(cond.reshape(2, KC, 128).transpose(2, 1, 0)),
    )
    return m


def _shared_inputs(inp):
    f = np.float32
    A = lambda x: np.ascontiguousarray(np.asarray(x, dtype=f))
    sh = dict(
        bmod_fm=A(inp["b_mod"].reshape(4, 144, 128).transpose(2, 0, 1)),
        gnorm_fm=A(inp["g_norm"].reshape(4, 3, KC, 128).transpose(3, 0, 1, 2)),
        gfin_fm=A(inp["g_final"].reshape(KC, 128).T),
        gq_fm=A(inp["g_q"].reshape(2, 4, 128).transpose(2, 0, 1)),
        gkv_fm=A(inp["g_kv"].reshape(2, 4, 128).transpose(2, 0, 1)),
        gsub_fm=A(inp["g_sub"].T),
        pscale_fm=A(inp["pool_scale"].reshape(2, 4, 128).transpose(2, 0, 1)),
        lamqk_b=A(np.broadcast_to(inp["lam_qk"].reshape(1, 2, 256), (128, 2, 256))),
        w_in_a=A(inp["w_in_a"]), w_q_up=A(inp["w_q_up"]), w_kv_up=A(inp["w_kv_up"]), w_fnet=A(inp["w_fnet"]),
        w_o_a=A(inp["w_o_a"]), w_in_d=A(inp["w_in_d"]), w_pool=A(inp["w_pool"]), w_o_d=A(inp["w_o_d"]),
    )
    for l in range(4):
        sh["w_mod%d" % l] = A(inp["w_mod"][l])
        for ff in range(2):
            sh["w_gate%d%d" % (l, ff)] = A(inp["w_ffn_gate"][l, ff])
            sh["w_up%d%d" % (l, ff)] = A(inp["w_ffn_up"][l, ff])
            sh["w_down%d%d" % (l, ff)] = A(inp["w_ffn_down"][l, ff])
    sh.update(_consts())
    return sh


def run(inp, NL=4, tiles=tuple(range(9)), cores=tuple(range(8)), trace=False, lmap=None):
    inp = {k: np.asarray(v) for k, v in inp.items()}
    nc = build(NL=NL, tiles=tiles, lmap=lmap)
    shared = _shared_inputs(inp)
    in_maps = [_core_inputs(inp, b, shared) for b in cores]
    res = run_bass_kernel_spmd(nc, in_maps, core_ids=list(range(len(cores))), trace=trace)
    return res


def kernel(**inputs):
    res = run(inputs)
    r = res.results
    y_prompt = np.concatenate([r[b]["y_p"].reshape(2, SP, D) for b in range(8)], 0)
    y_sample = np.stack([r[b]["y_s"] for b in range(8)], 0)
    ckv = np.concatenate([r[b]["o_ckv"] for b in range(8)], 0)
    kr = np.concatenate([r[b]["o_kr"] for b in range(8)], 0)
    dk = np.concatenate([r[b]["o_dk"] for b in range(8)], 0).reshape(16, 2, SP, 12, 2, 64)
    dv = np.concatenate([r[b]["o_dv"] for b in range(8)], 0).reshape(16, 2, SP, 12, 128)
    return (y_prompt.astype(np.float32), y_sample.astype(np.float32), ckv.astype(np.float32),
            kr.astype(np.float32), dk.astype(np.float32), dv.astype(np.float32))
```

```python
import math
import numpy as np
import concourse.bass as bass
import concourse.mybir as mybir
from concourse.bass_utils import run_bass_kernel_spmd

F32, BF16 = mybir.dt.float32, mybir.dt.bfloat16
ALU = mybir.AluOpType
AF = mybir.ActivationFunctionType
AX = mybir.AxisListType

D = 2048; KC = 16; T = 512; DFF = 5632; JC = 44; NMOD = 9
SS = 4096; SP = 256; PAST = 512; NTOK = 4608
EPS = 1e-6
MLA_SCALE = 1.0 / math.sqrt(192.0)
DIFF_SCALE = 1.0 / 8.0
KQ = 12
NWS = 5
WSZ = 4096


class Buf:
    def __init__(self, name, ap):
        self.name, self.ap, self.w, self.r = name, ap, None, []

    def __getitem__(self, idx):
        return V(self, self.ap[idx])


class V:
    def __init__(self, buf, ap):
        self.buf, self.ap = buf, ap

    def __getitem__(self, idx):
        return V(self.buf, self.ap[idx])

    def re(self, pat, **kw):
        return V(self.buf, self.ap.rearrange(pat, **kw))


class Op:
    __slots__ = ("eng", "fn", "deps", "sig", "cnt", "dma_i", "idx")

    def __init__(self, eng, fn):
        self.eng, self.fn, self.deps, self.sig, self.cnt, self.dma_i = eng, fn, [], False, 0, None


ENGS = ("pe", "act", "dve", "pool", "sp")
DMAQ = ("pool", "sp")


class Kern:
    def __init__(self, nc):
        self.nc = nc
        self.ops = {e: [] for e in ENGS}
        self.ndma = {q: 0 for q in DMAQ}
        self.dma_ops = {q: [] for q in DMAQ}
        self.bufs = []

    def sb(self, name, shape, dt):
        t = self.nc.alloc_sbuf_tensor(name, shape, dt)
        b = Buf(name, t.ap() if hasattr(t, "ap") else t[:])
        self.bufs.append(b)
        return b

    def psb(self, name, shape, dt=F32):
        t = self.nc.alloc_psum_tensor(name, shape, dt)
        b = Buf(name, t.ap() if hasattr(t, "ap") else t[:])
        self.bufs.append(b)
        return b

    def add(self, eng, fn, reads=(), writes=()):
        op = Op(eng, fn)
        op.idx = len(self.ops[eng])
        deps = []
        for b in reads:
            if b is not None and b.w is not None:
                deps.append(b.w)
        for b in writes:
            if b is None:
                continue
            if b.w is not None:
                deps.append(b.w)
            deps.extend(b.r)
        if eng in DMAQ:
            i = self.ndma[eng]
            op.dma_i = i
            self.ndma[eng] += 1
            if i >= KQ:
                deps.append(self.dma_ops[eng][i - KQ])
            self.dma_ops[eng].append(op)
        best = {}
        seen = set()
        for d in deps:
            if d is op or id(d) in seen:
                continue
            seen.add(id(d))
            if d.eng == "pe" and eng == "pe":
                continue
            if d.eng in DMAQ:
                if d.dma_i + KQ <= self.ndma[d.eng] - (1 if eng == d.eng else 0) and eng == d.eng:
                    pass
                d.sig = True
                op.deps.append(d)
            else:
                if d.eng not in best or best[d.eng].idx < d.idx:
                    best[d.eng] = d
        for d in best.values():
            d.sig = True
            op.deps.append(d)
        self.ops[eng].append(op)
        for b in writes:
            if b is not None:
                b.w = op
                b.r = []
        for b in reads:
            if b is not None and b not in writes:
                if eng in DMAQ:
                    b.r.append(op)
                else:
                    b.r = [r for r in b.r if r.eng != eng]
                    b.r.append(op)
        return op

    def barrier(self):
        lasts = []
        for e in ("pe", "act", "dve"):
            if self.ops[e]:
                lasts.append(self.ops[e][-1])
        for q in DMAQ:
            lasts.extend(self.dma_ops[q][-KQ:])
        for e in ENGS:
            op = Op(e, None)
            for d in lasts:
                if d.eng == e and e in ("pe",):
                    continue
                d.sig = True
                op.deps.append(d)
            self.ops[e].append(op)
        for b in self.bufs:
            b.w, b.r = None, []

    def mm(self, out, lhsT, rhs, start, stop):
        rd = [lhsT.buf, rhs.buf]
        self.add("pe", lambda e, o=out.ap, l=lhsT.ap, r=rhs.ap, s=start, p=stop: e.matmul(o, l, r, start=s, stop=p),
                 reads=rd, writes=[out.buf])

    def tr(self, out, in_, ident):
        self.add("pe", lambda e, o=out.ap, i=in_.ap, d=ident.ap: e.transpose(o, i, d),
                 reads=[in_.buf, ident.buf], writes=[out.buf])

    def act(self, out, in_, func, bias=None, scale=None):
        rd = [in_.buf]
        kw = {}
        if bias is not None:
            if isinstance(bias, V):
                rd.append(bias.buf); kw["bias"] = bias.ap
            else:
                kw["bias"] = bias
        if scale is not None:
            if isinstance(scale, V):
                rd.append(scale.buf); kw["scale"] = scale.ap
            else:
                kw["scale"] = scale
        self.add("act", lambda e, o=out.ap, i=in_.ap, f=func, kw=kw: e.activation(o, i, f, **kw),
                 reads=rd, writes=[out.buf])

    def tt(self, out, a, b, op, eng="dve"):
        self.add(eng, lambda e, o=out.ap, a=a.ap, b=b.ap, op=op: e.tensor_tensor(o, a, b, op),
                 reads=[a.buf, b.buf], writes=[out.buf])

    def ts(self, out, a, s1, s2, op0, op1=None, eng="dve"):
        rd = [a.buf]
        s1a = s1
        if isinstance(s1, V):
            rd.append(s1.buf); s1a = s1.ap
        s2a = s2
        if isinstance(s2, V):
            rd.append(s2.buf); s2a = s2.ap
        if op1 is None:
            self.add(eng, lambda e, o=out.ap, a=a.ap, s1a=s1a, op0=op0: e.tensor_scalar(o, a, s1a, None, op0),
                     reads=rd, writes=[out.buf])
        else:
            self.add(eng, lambda e, o=out.ap, a=a.ap, s1a=s1a, s2a=s2a, op0=op0, op1=op1:
                     e.tensor_scalar(o, a, s1a, s2a, op0, op1), reads=rd, writes=[out.buf])

    def stt(self, out, in0, scalar, in1, op0, op1, eng="dve"):
        rd = [in0.buf, in1.buf]
        sa = scalar
        if isinstance(scalar, V):
            rd.append(scalar.buf); sa = scalar.ap
        self.add(eng, lambda e, o=out.ap, a=in0.ap, sa=sa, b=in1.ap, op0=op0, op1=op1:
                 e.scalar_tensor_tensor(o, a, sa, b, op0, op1), reads=rd, writes=[out.buf])

    def copy(self, out, in_, eng="dve"):
        self.add(eng, lambda e, o=out.ap, i=in_.ap: e.tensor_copy(o, i), reads=[in_.buf], writes=[out.buf])

    def recip(self, out, in_):
        self.add("dve", lambda e, o=out.ap, i=in_.ap: e.reciprocal(o, i), reads=[in_.buf], writes=[out.buf])

    def memset(self, out, val, eng="dve"):
        self.add(eng, lambda e, o=out.ap, v=val: e.memset(o, v), writes=[out.buf])

    def reduce_sum(self, out, in_):
        self.add("dve", lambda e, o=out.ap, i=in_.ap: e.reduce_sum(o, i, AX.X), reads=[in_.buf], writes=[out.buf])

    def dma(self, q, out, in_):
        rd = [in_.buf] if isinstance(in_, V) else []
        wr = [out.buf] if isinstance(out, V) else []
        oa = out.ap if isinstance(out, V) else out
        ia = in_.ap if isinstance(in_, V) else in_
        self.add(q, lambda e, oa=oa, ia=ia: e.dma_start(out=oa, in_=ia), reads=rd, writes=wr)

    def emit(self):
        nc = self.nc
        ops = self.ops
        for e in ("pe", "act", "dve"):
            c = 0
            for op in ops[e]:
                if op.sig and op.fn is not None:
                    c += 1
                    op.cnt = c
        import contextlib
        with contextlib.ExitStack() as es:
            csem = {e: es.enter_context(nc.semaphore("s_" + e)) for e in ("pe", "act", "dve")}
            qsem = {q: [es.enter_context(nc.semaphore("q_%s%d" % (q, i))) for i in range(KQ)] for q in DMAQ}
            block = es.enter_context(nc.Block())

            def resolve(d):
                if d.eng in DMAQ:
                    return qsem[d.eng][d.dma_i % KQ], 16 * (d.dma_i // KQ + 1)
                return csem[d.eng], d.cnt

            def run(ename, e):
                waited = {}
                for op in ops[ename]:
                    for d in op.deps:
                        sem, val = resolve(d)
                        k = id(sem)
                        if waited.get(k, 0) >= val:
                            continue
                        waited[k] = val
                        e.wait_ge(sem, val)
                    if op.fn is None:
                        continue
                    ins = op.fn(e)
                    if ename in DMAQ:
                        ins.then_inc(qsem[ename][op.dma_i % KQ], 16)
                    elif op.sig:
                        ins.then_inc(csem[ename], 1)

            @block.tensor
            def _(e):
                run("pe", e)

            @block.scalar
            def _(e):
                run("act", e)

            @block.vector
            def _(e):
                run("dve", e)

            @block.gpsimd
            def _(e):
                run("pool", e)

            @block.sync
            def _(e):
                run("sp", e)


def _rope_tables():
    t = np.arange(SS)
    row, col = t // 64, t % 64
    inv = (10000.0 ** (-np.arange(16, dtype=np.float32) / 16)).astype(np.float32)
    cos = np.zeros((64, SS), np.float32); sin = np.zeros((64, SS), np.float32)
    R = np.zeros((64, 64), np.float32)
    for d in range(64):
        pos = row if d < 32 else col
        fi = (d % 32) % 16
        ang = (pos.astype(np.float32) * inv[fi]).astype(np.float32)
        cos[d] = np.cos(ang); sin[d] = np.sin(ang)
        if (d % 32) < 16:
            R[d, d + 16] = -1.0
        else:
            R[d, d - 16] = 1.0
    cos2 = np.concatenate([cos, cos], 0); sin2 = np.concatenate([sin, sin], 0)
    R2 = np.zeros((128, 128), np.float32); R2[:64, :64] = R; R2[64:, 64:] = R
    return cos2, sin2, np.ascontiguousarray(R2.T)


def _dft(n):
    k = np.arange(n, dtype=np.int64)
    ang = 2.0 * np.pi * ((k[:, None] * k[None, :]) % n).astype(np.float64) / n
    s = 1.0 / math.sqrt(n)
    return (np.cos(ang) * s).astype(np.float32), (np.sin(ang) * s).astype(np.float32)


def _invcnt(S):
    t = np.arange(S)
    out = np.zeros((4, S), np.float32)
    for g, w in enumerate((2, 4, 8, 16)):
        lo = np.clip(t - w // 2, 0, S); hi = np.clip(t + w - w // 2, 0, S)
        out[g] = 1.0 / (hi - lo).astype(np.float32)
    return out


_CONST_CACHE = {}


def _consts():
    if _CONST_CACHE:
        return _CONST_CACHE
    cos2, sin2, R2T = _rope_tables()
    c4096, s4096 = _dft(SS)
    c256, s256 = _dft(SP)
    c128, s128 = _dft(128)
    _CONST_CACHE.update(dict(
        k_ident=np.eye(128, dtype=np.float32), k_cos=cos2, k_sin=sin2, k_r2t=R2T,
        k_c4096=c4096, k_s4096=s4096,
        k_cs256=np.ascontiguousarray(np.stack([c256, s256], 0)),
        k_ccm=np.ascontiguousarray(np.concatenate([c128, -s128], 1)),
        k_inv_s=np.ascontiguousarray(np.broadcast_to(_invcnt(SS)[None], (128, 4, SS))),
        k_inv_p=np.ascontiguousarray(np.broadcast_to(_invcnt(SP)[None], (128, 4, SP))),
    ))
    return _CONST_CACHE


def build(NL=4, tiles=tuple(range(9)), lam_inits=None, lmap=None):
    nc = bass.Bass("TRN2", target_bir_lowering=False)
    K = Kern(nc)
    has_s = any(t < 8 for t in tiles)
    has_p = 8 in tiles
    s_tiles = [t for t in tiles if t < 8]

    def din(name, shape, dt=F32):
        return nc.dram_tensor(name, list(shape), dt, kind="ExternalInput").ap()

    def dout(name, shape, dt=F32):
        return nc.dram_tensor(name, list(shape), dt, kind="ExternalOutput").ap()

    def dscr(name, shape, dt):
        return nc.dram_tensor(name, list(shape), dt, kind="Internal").ap()

    xs_in = din("xs", [SS, D]); xp_in = din("xp", [2 * SP, D])
    c_ckv = din("c_ckv", [2, PAST, 512]); c_kr = din("c_kr", [2, PAST, 64])
    c_dk = din("c_dk", [2, PAST, 1536]); c_dv = din("c_dv", [2, PAST, 1536])
    cond_fm = din("cond_fm", [128, KC, 2])
    bmod_fm = din("bmod_fm", [128, 4, 144]); gnorm_fm = din("gnorm_fm", [128, 4, 3, KC])
    gfin_fm = din("gfin_fm", [128, KC]); gq_fm = din("gq_fm", [128, 2, 4]); gkv_fm = din("gkv_fm", [128, 2, 4])
    gsub_fm = din("gsub_fm", [128, 2]); pscale_fm = din("pscale_fm", [128, 2, 4]); lamqk_b = din("lamqk_b", [128, 2, 256])
    w_mod = [din("w_mod%d" % l, [D, NMOD * D]) for l in range(4)]
    w_gate = {(l, f): din("w_gate%d%d" % (l, f), [D, DFF]) for l in range(4) for f in range(2)}
    w_up = {(l, f): din("w_up%d%d" % (l, f), [D, DFF]) for l in range(4) for f in range(2)}
    w_down = {(l, f): din("w_down%d%d" % (l, f), [DFF, D]) for l in range(4) for f in range(2)}
    w_in_a = din("w_in_a", [2, D, 1600]); w_q_up = din("w_q_up", [2, 512, 2304]); w_kv_up = din("w_kv_up", [2, 512, 3072])
    w_fnet = din("w_fnet", [2, 4, 128, 128]); w_o_a = din("w_o_a", [2, D, D])
    w_in_d = din("w_in_d", [2, D, 5120]); w_pool = din("w_pool", [2, 4, 128, 128]); w_o_d = din("w_o_d", [2, D, D])
    k_ident = din("k_ident", [128, 128]); k_cos = din("k_cos", [128, SS]); k_sin = din("k_sin", [128, SS])
    k_r2t = din("k_r2t", [128, 128]); k_c4096 = din("k_c4096", [SS, SS]); k_s4096 = din("k_s4096", [SS, SS])
    k_cs256 = din("k_cs256", [2, SP, SP]); k_ccm = din("k_ccm", [128, 256])
    k_inv_s = din("k_inv_s", [128, 4, SS]); k_inv_p = din("k_inv_p", [128, 4, SP])
    y_p = dout("y_p", [2 * SP, D]); y_s = dout("y_s", [SS, D])
    o_ckv = dout("o_ckv", [2, 2, SP, 512]); o_kr = dout("o_kr", [2, 2, SP, 64])
    o_dk = dout("o_dk", [2, 2, SP, 1536]); o_dv = dout("o_dv", [2, 2, SP, 1536])
    x_scr = dscr("x_scr", [D, NTOK], F32); mix_scr = dscr("mix_scr", [D, NTOK], BF16)
    qn_scr = dscr("qn_scr", [1536, NTOK], BF16); qr_scr = dscr("qr_scr", [768, NTOK], BF16)
    ckv_scr = dscr("ckv_scr", [512, NTOK], BF16); kr_scr = dscr("kr_scr", [64, NTOK], BF16)
    fz_scr = dscr("fz_scr", [512, NTOK], BF16); pz_scr = dscr("pz_scr", [512, NTOK], F32)
    dq_scr = dscr("dq_scr", [1536, NTOK], BF16); dk_scr = dscr("dk_scr", [1536, NTOK], BF16)
    dv_scr = dscr("dv_scr", [NTOK, 1536], BF16)

    def fm(ap2):
        return ap2.rearrange("(c p) n -> p c n", p=128)

    ident = K.sb("ident", [128, 128], F32); r2t = K.sb("r2t", [128, 128], F32)
    onesb = K.sb("onesb", [128, 128], BF16)
    ones_d = K.sb("ones_d", [128, 128], BF16)
    ones_5 = K.sb("ones_5", [128, 128], BF16)
    ones_1 = K.sb("ones_1", [128, 128], BF16)
    epsb = K.sb("epsb", [128, 1], F32)
    modsb = K.sb("modsb", [128, 4, NMOD, KC, 2], F32)
    geff = K.sb("geff", [128, 4, 3, KC, 2], F32)
    gate = K.sb("gate", [128, 4, 3, KC, 2], F32)
    gnorm = K.sb("gnorm", [128, 4, 3, KC], F32); gfin = K.sb("gfin", [128, KC], F32)
    gq = K.sb("gq", [128, 2, 4], F32); gkv = K.sb("gkv", [128, 2, 4], F32)
    gsub = K.sb("gsub", [128, 2], F32); pscale = K.sb("pscale", [128, 2, 4], F32)
    lamsb = K.sb("lamsb", [128, 2, 8], F32)
    A32 = K.sb("A32", [128, 13312], F32)
    A16 = K.sb("A16", [128, 43008], BF16)
    wring = [K.sb("wr%d" % i, [128, WSZ], BF16) for i in range(NWS)]
    PS = [K.psb("ps%d" % i, [128, 512], F32) for i in range(8)]
    st = {"ps_a": 0, "ps_b": 0, "wr": 0}

    def psA():
        st["ps_a"] = (st["ps_a"] + 1) % 4
        return PS[st["ps_a"]]

    def psB():
        st["ps_b"] = (st["ps_b"] + 1) % 4
        return PS[4 + st["ps_b"]]

    def wload(src, nk, ncols):
        assert nk * ncols <= WSZ
        st["wr"] = (st["wr"] + 1) % NWS
        slot = wring[st["wr"]]
        P = src.shape[0]
        v = V(slot, slot.ap[0:P, 0:nk * ncols].rearrange("p (k n) -> p k n", k=nk))
        K.dma("pool", v, src)
        return v

    def a32(off, shape):
        n = int(np.prod(shape[1:]))
        assert off + n <= 13312
        v = V(A32, A32.ap[0:shape[0], off:off + n])
        if len(shape) == 3:
            v = v.re("p (a b) -> p a b", a=shape[1])
        return v

    def a16(off, shape):
        n = int(np.prod(shape[1:]))
        assert off + n <= 43008, (off, n)
        v = V(A16, A16.ap[0:shape[0], off:off + n])
        if len(shape) == 3:
            v = v.re("p (a b) -> p a b", a=shape[1])
        elif len(shape) == 4:
            v = v.re("p (a b c) -> p a b c", a=shape[1], b=shape[2])
        return v

    def sub(parent_v, name):
        b = Buf(name, parent_v.ap)
        K.bufs.append(b)
        return b

    K.dma("sp", ident[:], k_ident); K.dma("sp", r2t[:], k_r2t)
    K.dma("sp", gnorm[:], gnorm_fm); K.dma("sp", gfin[:], gfin_fm); K.dma("sp", gq[:], gq_fm)
    K.dma("sp", gkv[:], gkv_fm); K.dma("sp", gsub[:], gsub_fm); K.dma("sp", pscale[:], pscale_fm)
    K.memset(onesb[:], 1.0); K.memset(ones_d[:], 1.0 / 2048); K.memset(ones_5[:], 1.0 / 512)
    K.memset(ones_1[:], 1.0 / 128); K.memset(epsb[:], EPS)
    condf = sub(a32(0, [128, 32]), "condf"); bmod = sub(a32(64, [128, 576]), "bmod")
    scond = sub(a16(0, [128, 32]), "scond")
    lamq = sub(a32(1024, [128, 512]), "lamq"); lamt = sub(a32(2048, [128, 256]), "lamt")
    K.dma("sp", condf[:], cond_fm.rearrange("p k c -> p (k c)"))
    K.dma("sp", bmod[:], bmod_fm.rearrange("p l m -> p (l m)"))
    K.dma("sp", lamq[:], lamqk_b.rearrange("p l m -> p (l m)"))
    K.act(scond[:], condf[:], AF.Silu)
    scv = scond[:].re("p (k c) -> p k c", c=2)
    import os as _os
    for l in range(int(_os.environ.get('NMOD_DBG', NL))):
        pm = psB()
        wv = fm(w_mod[l])
        for mp in range(72):
            w = wload(wv[:, :, mp * 256:(mp + 1) * 256], KC, 256)
            for jj in range(2):
                m = mp * 2 + jj
                for kc in range(KC):
                    K.mm(pm[:, 2 * m:2 * m + 2], w[:, kc, jj * 128:(jj + 1) * 128], scv[:, kc, :], kc == 0, kc == KC - 1)
        pmv = pm[:, 0:288].re("p (m c) -> p m c", c=2)
        bv = bmod[:, l * 144:(l + 1) * 144]
        mv = modsb[:, l].re("p i k c -> p (i k) c")
        for c in range(2):
            K.tt(mv[:, :, c], pmv[:, :, c], bv, ALU.add)
        for n in range(3):
            for c in range(2):
                K.stt(geff[:, l, n, :, c], modsb[:, l, 3 * n + 1, :, c], 1.0, gnorm[:, l, n, :], ALU.add, ALU.mult)
                K.ts(gate[:, l, n, :, c], modsb[:, l, 3 * n + 2, :, c], (1.0 if n == 1 else 0.5), None, ALU.mult)
    if lam_inits is None:
        lam_inits = [0.8 - 0.6 * math.exp(-0.3 * l) for l in range(4)]
    for i in range(2):
        lq = lamq[:, i * 256:(i + 1) * 256]
        K.tt(lamt[:, 0:64], lq[:, 0:64], lq[:, 64:128], ALU.mult)
        K.tt(lamt[:, 64:128], lq[:, 128:192], lq[:, 192:256], ALU.mult)
        K.reduce_sum(lamsb[:, i, 1:2], lamt[:, 0:64]); K.reduce_sum(lamsb[:, i, 2:3], lamt[:, 64:128])
        K.act(lamsb[:, i, 3:5], lamsb[:, i, 1:3], AF.Exp)
        K.tt(lamsb[:, i, 5:6], lamsb[:, i, 4:5], lamsb[:, i, 3:4], ALU.subtract)
        K.ts(lamsb[:, i, 0:1], lamsb[:, i, 5:6], -lam_inits[2 * i + 1], None, ALU.add)
    K.barrier()

    xb = sub(a32(0, [128, KC, T]), "xb")
    rstd = sub(a32(8192, [128, T]), "rstd"); tmpA = sub(a32(8704, [128, T]), "tmpA")
    tmpB = sub(a32(9216, [128, T]), "tmpB"); sg0 = sub(a32(9728, [128, T]), "sg0"); sg1 = sub(a32(10240, [128, T]), "sg1")
    lat = sub(a32(10752, [128, 4, T]), "lat")
    stg32 = sub(a32(12800, [128, 512]), "stg32")
    hb = sub(a16(0, [128, KC, T]), "hb")
    hid = sub(a16(8192, [128, JC, T]), "hid")
    sq0 = sub(a16(30720, [128, T]), "sq0"); sq1 = sub(a16(31232, [128, T]), "sq1")
    stgA = hid[:, 0:12, :]; stgB = hid[:, 12:24, :]; qnb = hid[:, 24:28, :]
    cosT = K.sb("cosT", [128, T], F32); sinT = K.sb("sinT", [128, T], F32)
    wsm = K.sb("wsm", [128, 4, 128], BF16)
    sgs = [sg0, sg1]; sqs = [sq0, sq1]
    cnt = {"sg": 0, "sq": 0}

    def rms_stats(src, nch, ones_m, out_rstd):
        pss = psB()
        for c in range(nch):
            cnt["sq"] ^= 1
            sq = sqs[cnt["sq"]]
            K.tt(sq[:], src[:, c, :], src[:, c, :], ALU.mult)
            K.mm(pss[:], ones_m[:], sq[:], c == 0, c == nch - 1)
        K.act(tmpA[:], pss[:], AF.Sqrt, bias=epsb[:, 0:1])
        K.recip(out_rstd, tmpA[:])

    def norm_mod(l, n, c):
        rms_stats(xb[:], KC, ones_d, rstd[:])
        for kc in range(KC):
            K.tt(tmpB[:], xb[:, kc, :], rstd[:], ALU.mult)
            K.act(hb[:, kc, :], tmpB[:], AF.Identity, bias=modsb[:, l, 3 * n, kc, c:c + 1], scale=geff[:, l, n, kc, c:c + 1])

    def ffn(l, f, c):
        n = 0 if f == 0 else 2
        norm_mod(l, n, c)
        wg_v = fm(w_gate[l, f]); wu_v = fm(w_up[l, f]); wd_v = fm(w_down[l, f])
        for jp in range(JC // 2):
            wg = wload(wg_v[:, :, jp * 256:(jp + 1) * 256], KC, 256)
            wu = wload(wu_v[:, :, jp * 256:(jp + 1) * 256], KC, 256)
            for jj in range(2):
                j = jp * 2 + jj
                pg = psA(); pu = psA()
                for kc in range(KC):
                    K.mm(pg[:], wg[:, kc, jj * 128:(jj + 1) * 128], hb[:, kc, :], kc == 0, kc == KC - 1)
                for kc in range(KC):
                    K.mm(pu[:], wu[:, kc, jj * 128:(jj + 1) * 128], hb[:, kc, :], kc == 0, kc == KC - 1)
                cnt["sg"] ^= 1
                sg = sgs[cnt["sg"]]
                K.act(sg[:], pg[:], AF.Silu)
                K.tt(hid[:, j, :], sg[:], pu[:], ALU.mult)
        for mp in range(8):
            p = [psB(), psB()]
            for g in range(4):
                wd = wload(wd_v[:, g * 11:(g + 1) * 11, mp * 256:(mp + 1) * 256], 11, 256)
                for mi in range(2):
                    for jj in range(11):
                        j = g * 11 + jj
                        K.mm(p[mi][:], wd[:, jj, mi * 128:(mi + 1) * 128], hid[:, j, :], j == 0, j == JC - 1)
            for mi in range(2):
                m = mp * 2 + mi
                K.stt(xb[:, m, :], p[mi][:], gate[:, l, n, m, c:c + 1], xb[:, m, :], ALU.mult, ALU.add)

    xs_v = fm(x_scr)

    def load_x_input(t):
        src = xs_in if t < 8 else xp_in
        r0 = t * T if t < 8 else 0
        sv = lat[:].re("p a b -> p (a b)")
        for s in range(4):
            K.dma("sp", sv, src[r0 + s * 128:r0 + (s + 1) * 128, :])
            for g4 in range(4):
                pt = psA()
                for q in range(4):
                    kc = g4 * 4 + q
                    K.tr(pt[:, q * 128:(q + 1) * 128], sv[:, kc * 128:(kc + 1) * 128], ident[:])
                K.copy(xb[:, g4 * 4:(g4 + 1) * 4, s * 128:(s + 1) * 128], pt[:].re("p (q n) -> p q n", q=4))

    def load_x(t):
        K.dma("sp", xb[:], xs_v[:, :, t * T:(t + 1) * T])

    def store_x(t):
        K.dma("sp", xs_v[:, :, t * T:(t + 1) * T], xb[:])

    def tok_major_out(srcv, nch, dst_rows):
        P = srcv.ap.shape[0]
        for s in range(4):
            sv = V(stg32, stg32.ap[:, 0:512])
            done = 0
            while done < nch:
                nb = min(512 // P, nch - done)
                pt = psA()
                for q in range(nb):
                    K.tr(pt[:, q * P:(q + 1) * P], srcv[:, done + q, s * 128:(s + 1) * 128], ident[0:P, 0:P])
                K.copy(sv[:, 0:nb * P], pt[:, 0:nb * P])
                K.dma("sp", dst_rows(s)[:, done * P:(done + nb) * P], sv[:, 0:nb * P])
                done += nb

    def final_out(t, c):
        rms_stats(xb[:], KC, ones_d, rstd[:])
        dst = y_s if t < 8 else y_p
        r0 = t * T if t < 8 else 0
        for s in range(4):
            for g4 in range(4):
                pt = psA()
                for q in range(4):
                    kc = g4 * 4 + q
                    cnt["sg"] ^= 1
                    tb = sgs[cnt["sg"]]
                    K.tt(tmpB[:, 0:128], xb[:, kc, s * 128:(s + 1) * 128], rstd[:, s * 128:(s + 1) * 128], ALU.mult)
                    K.act(tb[:, 0:128], tmpB[:, 0:128], AF.Copy, scale=gfin[:, kc:kc + 1])
                    K.tr(pt[:, q * 128:(q + 1) * 128], tb[:, 0:128], ident[:])
                sv = V(stg32, stg32.ap[:, 0:512])
                K.copy(sv, pt[:])
                K.dma("sp", dst[r0 + s * 128:r0 + (s + 1) * 128, g4 * 512:(g4 + 1) * 512], sv)

    def rope_tables(t):
        K.dma("sp", cosT[:], k_cos[:, t * T:(t + 1) * T])
        K.dma("sp", sinT[:], k_sin[:, t * T:(t + 1) * T])

    def rope(dst_bf, src32, P, t):
        pr = psB()
        K.mm(pr[0:P, :], r2t[0:P, 0:P], src32, True, True)
        K.tt(tmpA[0:P, :], src32, cosT[0:P, :], ALU.mult)
        K.tt(tmpB[0:P, :], pr[0:P, :], sinT[0:P, :], ALU.mult)
        K.tt(dst_bf, tmpA[0:P, :], tmpB[0:P, :], ALU.add)

    def p1_mla(i, t, c):
        is_p = (t == 8)
        tok = slice(t * T, (t + 1) * T)
        wv = fm(w_in_a[i])
        wq = fm(w_q_up[i])
        if not is_p:
            rope_tables(t)

        def latent(base, g_fm, ones_m):
            for gp in range(2):
                w = wload(wv[:, :, base + gp * 256:base + (gp + 1) * 256], KC, 256)
                for jj in range(2):
                    p = psA()
                    for kc in range(KC):
                        K.mm(p[:], w[:, kc, jj * 128:(jj + 1) * 128], hb[:, kc, :], kc == 0, kc == KC - 1)
                    K.act(lat[:, gp * 2 + jj, :], p[:], AF.Copy)
            rms_stats(lat[:], 4, ones_5, rstd[:])
            for cc in range(4):
                K.tt(tmpB[:], lat[:, cc, :], rstd[:], ALU.mult)
                K.act(lat[:, cc, :], tmpB[:], AF.Copy, scale=g_fm[:, i, cc:cc + 1])

        latent(0, gq, ones_5)
        for cc in range(4):
            K.copy(qnb[:, cc, :], lat[:, cc, :])
        for hg in range(3):
            w = wload(wq[:, :, hg * 768:(hg + 1) * 768], 4, 768)
            for hh in range(4):
                h = hg * 4 + hh
                p = psA()
                for cc in range(4):
                    K.mm(p[:], w[:, cc, hh * 192:hh * 192 + 128], qnb[:, cc, :], cc == 0, cc == 3)
                K.act(stgA[:, h, :], p[:], AF.Copy)
                p2 = psA()
                for cc in range(4):
                    K.mm(p2[0:64, :], w[:, cc, hh * 192 + 128:hh * 192 + 192], qnb[:, cc, :], cc == 0, cc == 3)
                if is_p:
                    K.act(stgB[0:64, h, :], p2[0:64, :], AF.Copy)
                else:
                    cnt["sg"] ^= 1
                    q32 = sgs[cnt["sg"]]
                    K.act(q32[0:64, :], p2[0:64, :], AF.Copy)
                    rope(stgB[0:64, h, :], q32[0:64, :], 64, t)
        K.dma("sp", fm(qn_scr)[:, :, tok], stgA[:, 0:12, :])
        K.dma("sp", qr_scr.rearrange("(h p) n -> p h n", p=64)[:, :, tok], stgB[0:64, 0:12, :])
        latent(512, gkv, ones_5)
        for cc in range(4):
            K.copy(qnb[:, cc, :], lat[:, cc, :])
        K.dma("sp", fm(ckv_scr)[:, :, tok], qnb[:, 0:4, :])
        if is_p:
            tok_major_out(lat[:], 4, lambda s: o_ckv[s // 2, i, (s % 2) * 128:(s % 2 + 1) * 128, :])
        w = wload(wv[:, :, 1024:1088], KC, 64)
        p = psA()
        for kc in range(KC):
            K.mm(p[0:64, :], w[:, kc, :], hb[:, kc, :], kc == 0, kc == KC - 1)
        K.act(lat[0:64, 0, :], p[0:64, :], AF.Copy)
        if is_p:
            K.copy(stgA[0:64, 0, :], lat[0:64, 0, :])
            tok_major_out(lat[0:64, 0:1, :], 1, lambda s: o_kr[s // 2, i, (s % 2) * 128:(s % 2 + 1) * 128, :])
        else:
            rope(stgA[0:64, 0, :], lat[0:64, 0, :], 64, t)
        K.dma("sp", kr_scr[:, tok], stgA[0:64, 0, :])
        for gp in range(2):
            w = wload(wv[:, :, 1088 + gp * 256:1088 + (gp + 1) * 256], KC, 256)
            for jj in range(2):
                p = psA()
                for kc in range(KC):
                    K.mm(p[:], w[:, kc, jj * 128:(jj + 1) * 128], hb[:, kc, :], kc == 0, kc == KC - 1)
                K.act(stgB[:, gp * 2 + jj, :], p[:], AF.Copy)
        K.dma("sp", fm(fz_scr)[:, :, tok], stgB[:, 0:4, :])

    def p1_diff(i, t, c):
        is_p = (t == 8)
        tok = slice(t * T, (t + 1) * T)
        wv = fm(w_in_d[i])
        if not is_p:
            rope_tables(t)
        for gp in range(2):
            w = wload(wv[:, :, gp * 256:(gp + 1) * 256], KC, 256)
            for jj in range(2):
                p = psA()
                for kc in range(KC):
                    K.mm(p[:], w[:, kc, jj * 128:(jj + 1) * 128], hb[:, kc, :], kc == 0, kc == KC - 1)
                K.act(lat[:, gp * 2 + jj, :], p[:], AF.Copy)
        K.dma("sp", fm(pz_scr)[:, :, tok], lat[:, 0:4, :])
        for which, base, stg, scr in (("q", 512, stgA, dq_scr), ("k", 2048, stgB, dk_scr)):
            for gp in range(6):
                w = wload(wv[:, :, base + gp * 256:base + (gp + 1) * 256], KC, 256)
                for jj in range(2):
                    h = gp * 2 + jj
                    p = psA()
                    for kc in range(KC):
                        K.mm(p[:], w[:, kc, jj * 128:(jj + 1) * 128], hb[:, kc, :], kc == 0, kc == KC - 1)
                    if is_p:
                        K.act(stg[:, h, :], p[:], AF.Copy)
                        if which == "k":
                            K.act(lat[:, h % 4, :], p[:], AF.Copy)
                            if h % 4 == 3:
                                h0 = h - 3
                                tok_major_out(lat[:], 4, lambda s, h0=h0: o_dk[s // 2, i, (s % 2) * 128:(s % 2 + 1) * 128, h0 * 128:(h0 + 4) * 128])
                    else:
                        cnt["sg"] ^= 1
                        q32 = sgs[cnt["sg"]]
                        K.act(q32[:], p[:], AF.Copy)
                        rope(stg[:, h, :], q32[:], 128, t)
            K.dma("sp", fm(scr)[:, :, tok], stg[:, 0:12, :])
        for gp in range(0 if "vpath" not in _os.environ.get("SKIP_DBG", "") else 6, 6):
            w = wload(wv[:, :, 3584 + gp * 256:3584 + (gp + 1) * 256], KC, 256)
            for s in range(4):
                p = psA()
                for kc in range(KC):
                    K.mm(p[:, 0:256], hb[:, kc, s * 128:(s + 1) * 128], w[:, kc, :], kc == 0, kc == KC - 1)
                K.act(stgA[:, s, 0:256], p[:, 0:256], AF.Copy)
                if is_p:
                    sv = V(stg32, stg32.ap[:, 0:256])
                    K.act(sv, p[:, 0:256], AF.Copy)
                    K.dma("sp", o_dv[s // 2, i, (s % 2) * 128:(s % 2 + 1) * 128, gp * 256:(gp + 1) * 256], sv)
            K.dma("sp", dv_scr[t * T:(t + 1) * T, gp * 256:(gp + 1) * 256].rearrange("(s p) n -> p s n", p=128), stgA[:, 0:4, 0:256])

    def p3(l, t, c):
        i = l // 2
        wo = fm((w_o_a if l % 2 == 0 else w_o_d)[i])
        mixt = hid[:, 0:KC, :]
        K.dma("sp", mixt, fm(mix_scr)[:, :, t * T:(t + 1) * T])
        for mp in range(8):
            w = wload(wo[:, :, mp * 256:(mp + 1) * 256], KC, 256)
            for mi in range(2):
                m = mp * 2 + mi
                p = psB()
                for kc in range(KC):
                    K.mm(p[:], w[:, kc, mi * 128:(mi + 1) * 128], mixt[:, kc, :], kc == 0, kc == KC - 1)
                K.stt(xb[:, m, :], p[:], gate[:, l, 1, m, c:c + 1], xb[:, m, :], ALU.mult, ALU.add)

    def seqs_for(tiles_):
        out = []
        if has_s:
            out.append(dict(tok0=0, S=SS, cache=True, qt=T, sample=True))
        if has_p:
            out.append(dict(tok0=SS, S=SP, cache=False, qt=SP, sample=False))
            out.append(dict(tok0=SS + SP, S=SP, cache=False, qt=SP, sample=False))
        return out

    def softmax_av(nq, kts, score_fn, v_fn, pO, pS, ptile, scale):
        nk = len(kts)
        LA = 2
        pscs = {}

        def issue(ki):
            psc = psA()
            score_fn(psc, kts[ki])
            pscs[ki] = psc

        for ki in range(min(LA, nk)):
            issue(ki)
        for ki, kt in enumerate(kts):
            if ki + LA < nk:
                issue(ki + LA)
            psc = pscs.pop(ki)
            cnt["pt"] = (cnt.get("pt", 0) + 1) % 3
            pt = ptile[cnt["pt"]]
            K.act(pt[:, 0:nq], psc[:, 0:nq], AF.Exp, scale=scale)
            K.mm(pO[:, 0:nq], v_fn(kt), pt[:, 0:nq], ki == 0, ki == nk - 1)
            K.mm(pS[:, 0:nq], onesb[:], pt[:, 0:nq], ki == 0, ki == nk - 1)

    def p2_mla(i):
        ckvT = sub(a16(0, [128, 4, NTOK]), "ckvT")
        krT = sub(a16(18432, [128, NTOK]), "krT")
        knT = sub(a16(23040, [128, NTOK]), "knT")
        Vh = sub(a16(27648, [128, 36, 128]), "Vh")
        qnh = sub(a16(32256, [128, SS]), "qnh")
        qrh = sub(a16(36352, [128, SS]), "qrh")
        pts = [sub(a16(40448 + k * 512, [128, 512]), "pt%d" % k) for k in range(3)]
        osb = sub(a16(41984, [128, 512]), "osb")
        cst = sub(a32(0, [128, 4, 512]), "cst")
        rsum = sub(a32(2048, [128, 512]), "rsum")
        wkv = fm(w_kv_up[i])
        for sq_ in seqs_for(tiles):
            tok0, S, qt = sq_["tok0"], sq_["S"], sq_["qt"]
            nkeys = S + (PAST if sq_["cache"] else 0)
            nkt = nkeys // 128
            K.dma("sp", ckvT[:, :, 0:S], fm(ckv_scr)[:, :, tok0:tok0 + S])
            K.dma("sp", krT[0:64, 0:S], kr_scr[:, tok0:tok0 + S])
            if sq_["cache"]:
                K.dma("sp", cst[:], c_ckv[i].rearrange("(s p) f -> p s f", p=128))
                for s in range(4):
                    ptp = psA()
                    for cc in range(4):
                        K.tr(ptp[:, cc * 128:(cc + 1) * 128], cst[:, s, cc * 128:(cc + 1) * 128], ident[:])
                    K.copy(ckvT[:, :, S + s * 128:S + (s + 1) * 128], ptp[:].re("p (c n) -> p c n", c=4))
                K.dma("sp", cst[:, :, 0:64], c_kr[i].rearrange("(s p) f -> p s f", p=128))
                ptp = psA()
                for s in range(4):
                    K.tr(ptp[0:64, s * 128:(s + 1) * 128], cst[:, s, 0:64], ident[:])
                K.copy(krT[0:64, S:S + 512], ptp[0:64, :])
            for h in range(12):
                w = wload(wkv[:, :, h * 256:(h + 1) * 256], 4, 256)
                for k0 in range(0, nkeys, 512):
                    n = min(512, nkeys - k0)
                    p = psA()
                    for cc in range(4):
                        K.mm(p[:, 0:n], w[:, cc, 0:128], ckvT[:, cc, k0:k0 + n], cc == 0, cc == 3)
                    K.act(knT[:, k0:k0 + n], p[:, 0:n], AF.Copy)
                for k4 in range(0, nkt, 4):
                    nb = min(4, nkt - k4)
                    p = psA()
                    for q in range(nb):
                        kt = k4 + q
                        for cc in range(4):
                            K.mm(p[:, q * 128:(q + 1) * 128], ckvT[:, cc, kt * 128:(kt + 1) * 128], w[:, cc, 128:256], cc == 0, cc == 3)
                    K.copy(Vh[:, k4:k4 + nb, :], p[:, 0:nb * 128].re("p (q n) -> p q n", q=nb))
                K.dma("sp", qnh[:, 0:S], qn_scr[h * 128:(h + 1) * 128, tok0:tok0 + S])
                K.dma("sp", qrh[0:64, 0:S], qr_scr[h * 64:(h + 1) * 64, tok0:tok0 + S])
                for q0 in range(0, S, qt):
                    pO = psB(); pS = psB()

                    def score(psc, kt, q0=q0):
                        K.mm(psc[:, 0:qt], knT[:, kt * 128:(kt + 1) * 128], qnh[:, q0:q0 + qt], True, False)
                        K.mm(psc[:, 0:qt], krT[0:64, kt * 128:(kt + 1) * 128], qrh[0:64, q0:q0 + qt], False, True)

                    softmax_av(qt, list(range(nkt)), score, lambda kt: Vh[:, kt, :], pO, pS, pts, MLA_SCALE)
                    K.recip(rsum[:, 0:qt], pS[:, 0:qt])
                    K.tt(osb[:, 0:qt], pO[:, 0:qt], rsum[:, 0:qt], ALU.mult)
                    K.dma("sp", mix_scr[h * 128:(h + 1) * 128, tok0 + q0:tok0 + q0 + qt], osb[:, 0:qt])
        K.barrier()

    def p2_fourier(i):
        zT = sub(a16(0, [128, 2, SS]), "zT")
        ab = sub(a16(8192, [128, 32, 512]), "ab")
        fsb = sub(a16(24576, [128, 512]), "fsb"); osb = sub(a16(25088, [128, 512]), "osbf")
        K.dma("pool", wsm[:], w_fnet[i].rearrange("g c d -> c g d"))
        for sq_ in seqs_for(tiles):
            tok0, S = sq_["tok0"], sq_["S"]
            nsc = S // 128
            for gh in range(2):
                K.dma("sp", zT[:, :, 0:S], fm(fz_scr)[:, gh * 2:gh * 2 + 2, tok0:tok0 + S])
                for g2 in range(2):
                    for sc in range(nsc):
                        p = psA()
                        K.mm(p[:, 0:256], zT[:, g2, sc * 128:(sc + 1) * 128], ccm[:], True, True)
                        K.act(ab[:, sc, g2 * 256:(g2 + 1) * 256], p[:, 0:256], AF.Copy)
                nst = max(1, S // 512)
                stw = min(S, 512)
                for stile in range(nst):
                    pf = [psB(), psB()]
                    if sq_["sample"]:
                        tabs = (fm(k_c4096), fm(k_s4096))
                        for ti in range(2):
                            for gq_ in range(4):
                                w = wload(tabs[ti][:, gq_ * 8:(gq_ + 1) * 8, stile * 512:(stile + 1) * 512], 8, 512)
                                for g2 in range(2):
                                    for kk in range(8):
                                        sc = gq_ * 8 + kk
                                        K.mm(pf[g2][:], ab[:, sc, g2 * 256 + ti * 128:g2 * 256 + (ti + 1) * 128], w[:, kk, :],
                                             ti == 0 and sc == 0, ti == 1 and sc == 31)
                    else:
                        for ti in range(2):
                            for g2 in range(2):
                                for sc in range(2):
                                    K.mm(pf[g2][:, 0:stw], ab[:, sc, g2 * 256 + ti * 128:g2 * 256 + (ti + 1) * 128],
                                         cs256[:, ti * 2 + sc, :], ti == 0 and sc == 0, ti == 1 and sc == 1)
                    for g2 in range(2):
                        g = gh * 2 + g2
                        K.act(fsb[:, 0:stw], pf[g2][:, 0:stw], AF.Copy)
                        p = psA()
                        K.mm(p[:, 0:stw], wsm[:, g, :], fsb[:, 0:stw], True, True)
                        K.act(osb[:, 0:stw], p[:, 0:stw], AF.Copy)
                        K.dma("sp", mix_scr[(12 + g) * 128:(13 + g) * 128, tok0 + stile * 512:tok0 + stile * 512 + stw], osb[:, 0:stw])
        K.barrier()

    def p2_pool(i):
        W16 = SS + 16
        xp = sub(a32(0, [128, W16]), "xp"); sa = sub(a32(W16, [128, W16]), "sa")
        sbb = sub(a32(2 * W16, [128, W16]), "sbb")
        invf = sub(a32(3 * W16, [128, 976]), "invf")
        pooled = sub(a16(0, [128, SS]), "pooled")
        osb = sub(a16(SS, [128, 512]), "osbp")
        K.dma("pool", wsm[:], w_pool[i].rearrange("g c d -> c g d"))
        wp = wsm
        for sq_ in seqs_for(tiles):
            tok0, S = sq_["tok0"], sq_["S"]
            ktab = k_inv_s if sq_["sample"] else k_inv_p
            for g, wdw in enumerate((2, 4, 8, 16)):
                hw = wdw // 2
                K.memset(xp[:, 0:W16], 0.0)
                K.dma("sp", xp[:, hw:hw + S], pz_scr[g * 128:(g + 1) * 128, tok0:tok0 + S])
                cur = xp; L = S + wdw - 1
                step = 1
                nxt = [sa, sbb]; ni = 0
                while step < wdw:
                    L2 = L - step
                    K.tt(nxt[ni][:, 0:L2], cur[:, 0:L2], cur[:, step:step + L2], ALU.add)
                    cur = nxt[ni]; ni ^= 1; L = L2; step *= 2
                other = nxt[ni]
                for c0 in range(0, S, 976):
                    n = min(976, S - c0)
                    K.dma("sp", invf[:, 0:n], ktab[:, g, c0:c0 + n])
                    K.tt(other[:, c0:c0 + n], cur[:, c0:c0 + n], invf[:, 0:n], ALU.mult)
                K.tt(pooled[:, 0:S], other[:, 0:S], xp[:, hw:hw + S], ALU.subtract)
                for s0 in range(0, S, 512):
                    n = min(512, S - s0)
                    p = psA()
                    K.mm(p[:, 0:n], wp[:, g, :], pooled[:, s0:s0 + n], True, True)
                    K.act(osb[:, 0:n], p[:, 0:n], AF.Copy, scale=pscale[:, i, g:g + 1])
                    K.dma("sp", mix_scr[g * 128:(g + 1) * 128, tok0 + s0:tok0 + s0 + n], osb[:, 0:n])
        K.barrier()

    def p2_diff(i, lam_init):
        KT = sub(a16(0, [128, NTOK]), "KT")
        Vd = sub(a16(4608, [128, 36, 128]), "Vd")
        Qd = sub(a16(9216, [128, SS]), "Qd")
        pts = [sub(a16(13312 + k * 512, [128, 512]), "ptd%d" % k) for k in range(3)]
        osb = sub(a16(14848, [128, 512]), "osbd")
        sqd = sub(a16(15360, [128, 512]), "sqd")
        cst = sub(a32(0, [128, 4, 128]), "cstd")
        r0 = sub(a32(512, [128, 512]), "r0"); r1 = sub(a32(1024, [128, 512]), "r1")
        o0 = sub(a32(1536, [128, 512]), "o0"); o1 = sub(a32(2048, [128, 512]), "o1")
        rs = sub(a32(2560, [128, 512]), "rsd"); tq = sub(a32(3072, [128, 512]), "tq")
        gs = sub(a32(3584, [128, 2]), "gs")
        K.ts(gs[:, 0:1], gsub[:, i:i + 1], (1.0 - lam_init), None, ALU.mult)
        for sq_ in seqs_for(tiles):
            tok0, S, qt = sq_["tok0"], sq_["S"], sq_["qt"]
            nkeys = S + (PAST if sq_["cache"] else 0)
            nkt = nkeys // 128
            for h in range(12):
                K.dma("sp", KT[:, 0:S], dk_scr[h * 128:(h + 1) * 128, tok0:tok0 + S])
                K.dma("sp", Qd[:, 0:S], dq_scr[h * 128:(h + 1) * 128, tok0:tok0 + S])
                K.dma("sp", Vd[:, 0:S // 128, :], dv_scr[tok0:tok0 + S, h * 128:(h + 1) * 128].rearrange("(s p) n -> p s n", p=128))
                if sq_["cache"]:
                    K.dma("sp", cst[:], c_dk[i][:, h * 128:(h + 1) * 128].rearrange("(s p) f -> p s f", p=128))
                    ptp = psA()
                    for s in range(4):
                        K.tr(ptp[:, s * 128:(s + 1) * 128], cst[:, s, :], ident[:])
                    K.copy(KT[:, S:S + 512], ptp[:])
                    K.dma("pool", Vd[:, S // 128:S // 128 + 4, :], c_dv[i][:, h * 128:(h + 1) * 128].rearrange("(s p) f -> p s f", p=128))
                for q0 in range(0, S, qt):
                    pOs = []
                    for cidx in range(2):
                        pO = psB(); pS = psB()
                        lo = cidx * 64

                        def score(psc, kt, q0=q0, lo=lo):
                            K.mm(psc[:, 0:qt], KT[lo:lo + 64, kt * 128:(kt + 1) * 128], Qd[lo:lo + 64, q0:q0 + qt], True, True)

                        softmax_av(qt, list(range(nkt)), score, lambda kt: Vd[:, kt, :], pO, pS, pts, DIFF_SCALE)
                        pOs.append((pO, pS))
                    K.recip(r0[:, 0:qt], pOs[0][1][:, 0:qt]); K.recip(r1[:, 0:qt], pOs[1][1][:, 0:qt])
                    K.tt(o0[:, 0:qt], pOs[0][0][:, 0:qt], r0[:, 0:qt], ALU.mult)
                    K.tt(o1[:, 0:qt], pOs[1][0][:, 0:qt], r1[:, 0:qt], ALU.mult)
                    K.stt(o0[:, 0:qt], o1[:, 0:qt], lamsb[:, i, 0:1], o0[:, 0:qt], ALU.mult, ALU.add)
                    K.tt(sqd[:, 0:qt], o0[:, 0:qt], o0[:, 0:qt], ALU.mult)
                    pn = psA()
                    K.mm(pn[:, 0:qt], ones_1[:], sqd[:, 0:qt], True, True)
                    K.act(tq[:, 0:qt], pn[:, 0:qt], AF.Sqrt, bias=epsb[:, 0:1])
                    K.recip(rs[:, 0:qt], tq[:, 0:qt])
                    K.tt(o1[:, 0:qt], o0[:, 0:qt], rs[:, 0:qt], ALU.mult)
                    K.act(osb[:, 0:qt], o1[:, 0:qt], AF.Copy, scale=gs[:, 0:1])
                    K.dma("sp", mix_scr[(4 + h) * 128:(5 + h) * 128, tok0 + q0:tok0 + q0 + qt], osb[:, 0:qt])
        K.barrier()

    ccm = K.sb("ccm", [128, 256], BF16)
    cs256 = K.sb("cs256", [128, 4, 256], BF16)
    K.dma("pool", ccm[:], k_ccm)
    K.dma("pool", cs256[:], k_cs256.rearrange("t (s p) n -> p (t s) n", p=128))
    K.barrier()

    LM = list(lmap) if lmap is not None else list(range(NL))
    for li in range(NL):
        l = LM[li]
        i = l // 2
        for t in tiles:
            c = 0 if t < 8 else 1
            if li == 0:
                load_x_input(t)
            else:
                lp = LM[li - 1]
                load_x(t)
                p3(lp, t, c)
                ffn(lp, 1, c)
            ffn(l, 0, c)
            norm_mod(l, 1, c)
            if l % 2 == 0:
                p1_mla(i, t, c)
            else:
                p1_diff(i, t, c)
            store_x(t)
        K.barrier()
        _skip = _os.environ.get("SKIP_DBG", "").split(",")
        if l % 2 == 0:
            if "p2mla" not in _skip:
                p2_mla(i)
            if "p2four" not in _skip:
                p2_fourier(i)
        else:
            if "p2pool" not in _skip:
                p2_pool(i)
            if "p2diff" not in _skip:
                p2_diff(i, lam_inits[l])
    l = LM[NL - 1]
    for t in tiles:
        c = 0 if t < 8 else 1
        load_x(t)
        p3(l, t, c)
        ffn(l, 1, c)
        final_out(t, c)
    K.barrier()
    K.emit()
    return nc


def _core_inputs(inp, b, shared):
    f = np.float32
    cond = np.stack([inp["c"][b], inp["c_ctx"]], 0).astype(f)
    m = dict(shared)
    m.update(
        xs=np.ascontiguousarray(inp["x_sample"][b]),
        xp=np.ascontiguousarray(inp["x_prompt"][2 * b:2 * b + 2].reshape(2 * SP, D)),
        c_ckv=np.ascontiguousarray(inp["cache_mla_ckv"][b]),
        c_kr=np.ascontiguousarray(inp["cache_mla_krope"][b]),
        c_dk=np.ascontiguousarray(inp["cache_diff_k"][b].reshape(2, PAST, 1536)),
        c_dv=np.ascontiguousarray(inp["cache_diff_v"][b].reshape(2, PAST, 1536)),
        cond_fm=np.ascontiguousarray(cond.reshape(2, KC, 128).transpose(2, 1, 0)),
    )
    return m


def _shared_inputs(inp):
    f = np.float32
    A = lambda x: np.ascontiguousarray(np.asarray(x, dtype=f))
    sh = dict(
        bmod_fm=A(inp["b_mod"].reshape(4, 144, 128).transpose(2, 0, 1)),
        gnorm_fm=A(inp["g_norm"].reshape(4, 3, KC, 128).transpose(3, 0, 1, 2)),
        gfin_fm=A(inp["g_final"].reshape(KC, 128).T),
        gq_fm=A(inp["g_q"].reshape(2, 4, 128).transpose(2, 0, 1)),
        gkv_fm=A(inp["g_kv"].reshape(2, 4, 128).transpose(2, 0, 1)),
        gsub_fm=A(inp["g_sub"].T),
        pscale_fm=A(inp["pool_scale"].reshape(2, 4, 128).transpose(2, 0, 1)),
        lamqk_b=A(np.broadcast_to(inp["lam_qk"].reshape(1, 2, 256), (128, 2, 256))),
        w_in_a=A(inp["w_in_a"]), w_q_up=A(inp["w_q_up"]), w_kv_up=A(inp["w_kv_up"]), w_fnet=A(inp["w_fnet"]),
        w_o_a=A(inp["w_o_a"]), w_in_d=A(inp["w_in_d"]), w_pool=A(inp["w_pool"]), w_o_d=A(inp["w_o_d"]),
    )
    for l in range(4):
        sh["w_mod%d" % l] = A(inp["w_mod"][l])
        for ff in range(2):
            sh["w_gate%d%d" % (l, ff)] = A(inp["w_ffn_gate"][l, ff])
            sh["w_up%d%d" % (l, ff)] = A(inp["w_ffn_up"][l, ff])
            sh["w_down%d%d" % (l, ff)] = A(inp["w_ffn_down"][l, ff])
    sh.update(_consts())
    return sh


def run(inp, NL=4, tiles=tuple(range(9)), cores=tuple(range(8)), trace=False, lmap=None):
    inp = {k: np.asarray(v) for k, v in inp.items()}
    nc = build(NL=NL, tiles=tiles, lmap=lmap)
    shared = _shared_inputs(inp)
    in_maps = [_core_inputs(inp, b, shared) for b in cores]
    res = run_bass_kernel_spmd(nc, in_maps, core_ids=list(range(len(cores))), trace=trace)
    return res


def kernel(**inputs):
    res = run(inputs)
    r = res.results
    y_prompt = np.concatenate([r[b]["y_p"].reshape(2, SP, D) for b in range(8)], 0)
    y_sample = np.stack([r[b]["y_s"] for b in range(8)], 0)
    ckv = np.concatenate([r[b]["o_ckv"] for b in range(8)], 0)
    kr = np.concatenate([r[b]["o_kr"] for b in range(8)], 0)
    dk = np.concatenate([r[b]["o_dk"] for b in range(8)], 0).reshape(16, 2, SP, 12, 2, 64)
    dv = np.concatenate([r[b]["o_dv"] for b in range(8)], 0).reshape(16, 2, SP, 12, 128)
    return (y_prompt.astype(np.float32), y_sample.astype(np.float32), ckv.astype(np.float32),
            kr.astype(np.float32), dk.astype(np.float32), dv.astype(np.float32))
```
